# Optimizing a Trainium2 kernel written in Bass

```python
import jax, jax.numpy as jnp
from jax import lax
import numpy as np

D_MODEL = 1024
BATCH = 4
SEQ = 4096
DEPTH = 4

N_MIXERS = 3
SHORT_CONV_W = 3
POOL_WINDOWS = (2, 4, 8, 16)
N_POOL_GROUPS = len(POOL_WINDOWS)
POOL_GROUP_W = D_MODEL // N_POOL_GROUPS
CONF_CONV_W = 31
D_FF = ((8 * D_MODEL // 3 + 255) // 256) * 256
RMS_EPS = 1e-6
LN_EPS = 1e-5

kernel_name = "hybrid_interleaved_conv_pool_conformer"


def rmsnorm(x, g):
    xf = x.astype(jnp.float32)
    y = xf * lax.rsqrt(jnp.mean(xf * xf, axis=-1, keepdims=True) + RMS_EPS)
    return (y * g.astype(jnp.float32)).astype(x.dtype)


def layernorm(x, g, b):
    xf = x.astype(jnp.float32)
    mu = jnp.mean(xf, axis=-1, keepdims=True)
    xc = xf - mu
    var = jnp.mean(xc * xc, axis=-1, keepdims=True)
    y = xc * lax.rsqrt(var + LN_EPS) * g.astype(jnp.float32) + b.astype(jnp.float32)
    return y.astype(x.dtype)


def causal_depthwise_conv(u, w):
    k, c = w.shape
    return lax.conv_general_dilated(
        u, w[:, None, :].astype(u.dtype),
        window_strides=(1,), padding=((k - 1, 0),),
        dimension_numbers=("NWC", "WIO", "NWC"),
        feature_group_count=c)


def short_conv_mixer(u, w_in, conv_w, w_out):
    b, c, v = jnp.split(u @ w_in, 3, axis=-1)
    return (b * causal_depthwise_conv(c * v, conv_w)) @ w_out


def multiscale_pool_mixer(u, w_grp, scale):
    bn, s, d = u.shape
    ug = u.reshape(bn, s, N_POOL_GROUPS, POOL_GROUP_W).astype(jnp.float32)
    cs = jnp.cumsum(ug, axis=1)
    t = jnp.arange(s)
    pooled = []
    for g, w in enumerate(POOL_WINDOWS):
        c = cs[:, :, g, :]
        lag = jnp.pad(c, ((0, 0), (w, 0), (0, 0)))[:, :s]
        cnt = jnp.minimum(t + 1, w).astype(jnp.float32)[None, :, None]
        pooled.append((c - lag) / cnt)
    mixed = (jnp.stack(pooled, axis=2) - ug).astype(u.dtype)
    y = jnp.einsum("bsgc,gcd->bsgd", mixed, w_grp)
    return y.reshape(bn, s, d) * scale


def conformer_conv_module(u, w_pw1, b_pw1, w_dw, b_dw, ln_g, ln_b, w_pw2, b_pw2):
    a, gate = jnp.split(u @ w_pw1 + b_pw1, 2, axis=-1)
    h = a * jax.nn.sigmoid(gate)
    h = causal_depthwise_conv(h, w_dw) + b_dw
    h = jax.nn.silu(layernorm(h, ln_g, ln_b))
    return h @ w_pw2 + b_pw2


def swiglu_ffn(u, w_gu, w_down):
    g, up = jnp.split(u @ w_gu, 2, axis=-1)
    return (jax.nn.silu(g) * up) @ w_down


def setup_inputs(seed: int = 0) -> dict:
    key = jax.random.key(seed)
    keys = iter(jax.random.split(key, 64))
    d, f = D_MODEL, D_FF

    def nrm(shape, scale):
        return jax.random.normal(next(keys), shape, jnp.float32) * scale

    def gain(n):
        return 1.0 + nrm((n,), 0.05)

    def short_conv_params(i):
        return {f"a{i}_w_in": nrm((d, 3 * d), d ** -0.5),
                f"a{i}_conv": nrm((SHORT_CONV_W, d), SHORT_CONV_W ** -0.5),
                f"a{i}_w_out": nrm((d, d), d ** -0.5)}

    def ffn_params(i):
        return {f"ln2_{i}": gain(d),
                f"ffn{i}_w_gu": nrm((d, 2 * f), d ** -0.5),
                f"ffn{i}_w_down": nrm((f, d), f ** -0.5)}

    p = {"x": nrm((BATCH, SEQ, d), 1.0)}
    p["ln1_0"] = gain(d)
    p.update(short_conv_params(0))
    p.update(ffn_params(0))
    p["ln1_1"] = gain(d)
    p["b1_w_grp"] = nrm((N_POOL_GROUPS, POOL_GROUP_W, POOL_GROUP_W), POOL_GROUP_W ** -0.5)
    p["b1_scale"] = 1.0 + nrm((d,), 0.1)
    p.update(ffn_params(1))
    p["ln1_2"] = gain(d)
    p["c2_w_pw1"] = nrm((d, 2 * d), d ** -0.5)
    p["c2_b_pw1"] = nrm((2 * d,), 0.02)
    p["c2_dw"] = nrm((CONF_CONV_W, d), CONF_CONV_W ** -0.5)
    p["c2_b_dw"] = nrm((d,), 0.02)
    p["c2_ln_g"] = gain(d)
    p["c2_ln_b"] = nrm((d,), 0.02)
    p["c2_w_pw2"] = nrm((d, d), d ** -0.5)
    p["c2_b_pw2"] = nrm((d,), 0.02)
    p.update(ffn_params(2))
    p["ln1_3"] = gain(d)
    p.update(short_conv_params(3))
    p.update(ffn_params(3))
    p["ln_f"] = gain(d)
    return p


def reference(x,
              ln1_0, a0_w_in, a0_conv, a0_w_out, ln2_0, ffn0_w_gu, ffn0_w_down,
              ln1_1, b1_w_grp, b1_scale, ln2_1, ffn1_w_gu, ffn1_w_down,
              ln1_2, c2_w_pw1, c2_b_pw1, c2_dw, c2_b_dw, c2_ln_g, c2_ln_b, c2_w_pw2, c2_b_pw2,
              ln2_2, ffn2_w_gu, ffn2_w_down,
              ln1_3, a3_w_in, a3_conv, a3_w_out, ln2_3, ffn3_w_gu, ffn3_w_down,
              ln_f):
    mixer_fns = (short_conv_mixer, multiscale_pool_mixer, conformer_conv_module)
    mixer_params = (
        (a0_w_in, a0_conv, a0_w_out),
        (b1_w_grp, b1_scale),
        (c2_w_pw1, c2_b_pw1, c2_dw, c2_b_dw, c2_ln_g, c2_ln_b, c2_w_pw2, c2_b_pw2),
        (a3_w_in, a3_conv, a3_w_out),
    )
    pre_mix_norms = (ln1_0, ln1_1, ln1_2, ln1_3)
    pre_ffn_norms = (ln2_0, ln2_1, ln2_2, ln2_3)
    ffn_params = ((ffn0_w_gu, ffn0_w_down), (ffn1_w_gu, ffn1_w_down),
                  (ffn2_w_gu, ffn2_w_down), (ffn3_w_gu, ffn3_w_down))

    h = x
    for i in range(DEPTH):
        mixer = mixer_fns[i % N_MIXERS]
        h = h + mixer(rmsnorm(h, pre_mix_norms[i]), *mixer_params[i])
        h = h + swiglu_ffn(rmsnorm(h, pre_ffn_norms[i]), *ffn_params[i])
    return rmsnorm(h, ln_f)
```

```python
import numpy as np
from contextlib import ExitStack
import concourse.bass as bass
import concourse.mybir as mybir
from concourse.bass_utils import run_bass_kernel_spmd

F32 = mybir.dt.float32
BF16 = mybir.dt.bfloat16
ALU = mybir.AluOpType
AF = mybir.ActivationFunctionType

D = 1024
KC = 8
FF = 2816
FC = 22
SEQ = 4096
BATCH = 4
TCORE = 2048
HALO = 64
T = TCORE + HALO
PAD = 32
ZT = T + PAD
TILES = [(0, HALO)] + [(HALO + 512 * i, 512) for i in range(4)]
NT = len(TILES)
POOL_W = (2, 4, 8, 16)
NTAB = HALO + 16
CONFK = 31
FGROUPS = [(0, 8), (8, 15), (15, 22)]

_PL = {}
_off = 0
for _n, _c in ([(f"ln1_{i}", 8) for i in range(4)] + [(f"ln2_{i}", 8) for i in range(4)] + [("ln_f", 8)] +
               [("a0_conv", 24), ("a3_conv", 24), ("b1_scale", 8), ("c2_b_pw1", 16), ("c2_dw", 8 * CONFK),
                ("c2_b_dw", 8), ("c2_ln_g", 8), ("c2_ln_b", 8), ("c2_b_pw2", 8), ("eps_rms", 1), ("eps_ln", 1),
                ("mask", HALO), ("tab", 4 * NTAB), ("ident", 128)]):
    _PL[_n] = (_off, _c)
    _off += _c
NPAR = _off

WSHAPES = []
for _i in (0, 3):
    WSHAPES += [(f"a{_i}_w_in", [D, 3 * D]), (f"a{_i}_w_out", [D, D])]
WSHAPES += [("b1_w_grp", [4, 256, 256]), ("c2_w_pw1", [D, 2 * D]), ("c2_w_pw2", [D, D])]
for _i in range(4):
    WSHAPES += [(f"ffn{_i}_w_gu", [D, 2 * FF]), (f"ffn{_i}_w_down", [FF, D])]


def _vec(v, n):
    return np.ascontiguousarray(np.asarray(v, np.float32).reshape(n, 128).T)


def build_params(inp, half):
    P = np.zeros((128, NPAR), np.float32)

    def put(name, arr):
        o, c = _PL[name]
        P[:, o:o + c] = np.asarray(arr, np.float32).reshape(128, c)

    for i in range(4):
        put(f"ln1_{i}", _vec(inp[f"ln1_{i}"], 8))
        put(f"ln2_{i}", _vec(inp[f"ln2_{i}"], 8))
    put("ln_f", _vec(inp["ln_f"], 8))
    for nm in ("a0_conv", "a3_conv"):
        w = np.asarray(inp[nm], np.float32)
        put(nm, w.reshape(3, 8, 128).transpose(2, 1, 0))
    put("b1_scale", _vec(inp["b1_scale"], 8))
    put("c2_b_pw1", _vec(inp["c2_b_pw1"], 16))
    put("c2_dw", np.asarray(inp["c2_dw"], np.float32).reshape(CONFK, 8, 128).transpose(2, 1, 0))
    for nm in ("c2_b_dw", "c2_ln_g", "c2_ln_b", "c2_b_pw2"):
        put(nm, _vec(inp[nm], 8))
    put("eps_rms", np.full((128, 1), 1e-6, np.float32))
    put("eps_ln", np.full((128, 1), 1e-5, np.float32))
    put("mask", np.full((128, HALO), 1.0 if half == 1 else 0.0, np.float32))
    tab = np.zeros((4, NTAB), np.float32)
    for g, w in enumerate(POOL_W):
        for j in range(NTAB):
            t_abs = half * TCORE + j - HALO
            cnt = min(t_abs + 1, w) if t_abs >= 0 else w
            tab[g, j] = 1.0 / cnt
    put("tab", np.broadcast_to(tab.reshape(1, 4 * NTAB), (128, 4 * NTAB)))
    put("ident", np.eye(128, dtype=np.float32))
    return P


class Sched:
    def __init__(self, nc):
        self.nc = nc
        self.ops = []
        self.last_w = {}
        self.readers = {}
        self.phase = 0

    def op(self, eng, fn, reads=(), writes=(), dma=None):
        i = len(self.ops)
        deps = set()
        for r in reads:
            if r in self.last_w:
                deps.add(self.last_w[r])
        for r in writes:
            if r in self.last_w:
                deps.add(self.last_w[r])
            rd = self.readers.get(r)
            if rd:
                deps.update(rd.values())
        key = eng if dma is None else ("dma", i)
        for r in reads:
            self.readers.setdefault(r, {})[key] = i
        for r in writes:
            self.last_w[r] = i
            self.readers[r] = {}
        deps.discard(i)
        if eng == "pe":
            deps = {d for d in deps if not (self.ops[d]["eng"] == "pe" and self.ops[d]["dma"] is None)}
        self.ops.append(dict(eng=eng, fn=fn, deps=deps, dma=dma, need=False, phase=self.phase))
        return i

    def emit(self):
        nc = self.nc
        ops = self.ops
        for o in ops:
            for d in o["deps"]:
                ops[d]["need"] = True
            if o["dma"] is not None:
                o["need"] = True
        last_on = {}
        for i, o in enumerate(ops):
            last_on[o["eng"]] = i
        for i in last_on.values():
            ops[i]["need"] = True
        cnt = {}
        for o in ops:
            if not o["need"]:
                o["ev"] = None
                continue
            if o["dma"] is not None:
                s = "d_" + o["dma"]
                cnt[s] = cnt.get(s, 0) + 16
            else:
                s = "e%d_%s" % (o["phase"], o["eng"])
                cnt[s] = cnt.get(s, 0) + 1
            o["ev"] = (s, cnt[s])
        self.counts = dict(cnt)
        with ExitStack() as es:
            sems = {s: es.enter_context(nc.semaphore(s)) for s in sorted(cnt)}
            block = es.enter_context(nc.Block())

            def make(engname):
                def body(eng):
                    waited = {}
                    for o in ops:
                        if o["eng"] != engname:
                            continue
                        need = {}
                        for d in o["deps"]:
                            s, v = ops[d]["ev"]
                            if v > need.get(s, 0):
                                need[s] = v
                        for s, v in need.items():
                            if v > waited.get(s, 0):
                                eng.wait_ge(sems[s], v)
                                waited[s] = v
                        ins = o["fn"](eng)
                        if o["ev"] is not None:
                            ins.then_inc(sems[o["ev"][0]], 16 if o["dma"] is not None else 1)
                    if engname == "sp":
                        for s, v in cnt.items():
                            if v > waited.get(s, 0):
                                eng.wait_ge(sems[s], v)
                return body

            block.tensor(make("pe"))
            block.scalar(make("act"))
            block.vector(make("dve"))
            block.gpsimd(make("pool"))
            block.sync(make("sp"))


class Rot:
    def __init__(self, name, bufs):
        self.name, self.bufs, self.i = name, bufs, 0

    def next(self):
        j = self.i % len(self.bufs)
        self.i += 1
        return (self.name, j), self.bufs[j]


def f_mm(out, lhsT, rhs, start, stop):
    return lambda e: e.matmul(out, lhsT=lhsT, rhs=rhs, start=start, stop=stop)


def f_act(out, in_, func, bias=None, scale=None):
    kw = {}
    if bias is not None:
        kw["bias"] = bias
    if scale is not None:
        kw["scale"] = scale
    return lambda e: e.activation(out=out, in_=in_, func=func, **kw)


def f_tt(out, in0, in1, op):
    return lambda e: e.tensor_tensor(out=out, in0=in0, in1=in1, op=op)


def f_stt(out, in0, scalar, in1, op0, op1):
    return lambda e: e.scalar_tensor_tensor(out=out, in0=in0, scalar=scalar, in1=in1, op0=op0, op1=op1)


def f_copy(out, in_):
    return lambda e: e.tensor_copy(out=out, in_=in_)


def f_recip(out, in_):
    return lambda e: e.reciprocal(out=out, in_=in_)


def f_memset(ap, v):
    return lambda e: e.memset(ap, v)


def f_dma(out, in_):
    return lambda e: e.dma_start(out=out, in_=in_)


def build_program(debug=False, nlayers=4):
    nc = bass.Bass("TRN2", target_bir_lowering=False)
    xT = nc.dram_tensor("xT", [D, T], F32, kind="ExternalInput").ap()
    par = nc.dram_tensor("par", [128, NPAR], F32, kind="ExternalInput").ap()
    W = {n: nc.dram_tensor(n, s, F32, kind="ExternalInput").ap() for n, s in WSHAPES}
    yT = nc.dram_tensor("yT", [D, TCORE], F32, kind="ExternalOutput").ap()
    dbg = nc.dram_tensor("dbg", [8, D, T], F32, kind="ExternalOutput").ap() if debug else None

    with ExitStack() as es:
        def sb(name, shape, dt):
            return es.enter_context(nc.sbuf_tensor(name, shape, dt))

        H = sb("H", [128, KC, T], F32)
        Uf = sb("U", [128, KC * T], BF16)
        Zf = sb("Z", [128, KC * ZT], BF16)
        WS = [sb(f"ws{i}", [128, 4096], BF16) for i in range(4)]
        PAR = sb("PAR", [128, NPAR], F32)
        ONES = sb("ones", [128, 128], BF16)
        IDB = sb("idb", [128, 128], BF16)
        TMPb = [sb(f"tmp{i}", [128, 544], F32) for i in range(5)]
        SQb = [sb(f"sq{i}", [128, 512], BF16) for i in range(3)]
        RSb = [sb(f"rs{i}", [128, 512], F32) for i in range(2)]
        DGb = [sb(f"dg{i}", [128, CONFK, 128], BF16) for i in range(2)]
        SCR = sb("scr", [128, 8], F32)
        PS = {n: es.enter_context(nc.psum_tensor("ps" + n, [128, 512], F32))
              for n in ("G0", "G1", "U0", "U1", "D0", "D1", "N0", "N1")}

        Uv = Uf[:].rearrange("p (k t) -> p k t", k=KC)
        Zv = Zf[:].rearrange("p (k t) -> p k t", k=KC)
        UB = [Uf[:, i * 2 * T:(i + 1) * 2 * T].bitcast(F32) for i in range(4)]
        YT = [WS[2][:].bitcast(F32), WS[3][:].bitcast(F32)]

        def zt(k, c0, n):
            return Zv[:, k, PAD + c0:PAD + c0 + n]

        def pc(name, c=0, n=1):
            o, _ = _PL[name]
            return PAR[:, o + c:o + c + n]

        S = Sched(nc)
        TMP = Rot("tmp", TMPb)
        SQ = Rot("sq", SQb)
        RS = Rot("rs", RSb)
        DG = Rot("dg", DGb)
        NB = Rot("psN", [PS["N0"], PS["N1"]])
        slot_i = [0]

        def next_slot():
            s = slot_i[0] % 4
            slot_i[0] += 1
            return s

        HALL = lambda k: [("h", k, ti) for ti in range(NT)]
        UALL = [("u", k, ti) for k in range(KC) for ti in range(NT)]

        S.op("sp", f_dma(PAR[:], par), writes=["par"], dma="par")
        for k in range(KC):
            S.op("sp", f_dma(H[:, k, :], xT[k * 128:(k + 1) * 128, :]), writes=HALL(k), dma=f"x{k}")
        S.op("dve", f_memset(ONES[:], 1.0 / D), writes=["ones"])
        S.op("dve", f_copy(IDB[:], pc("ident", 0, 128)), reads=["par"], writes=["idb"])
        S.op("dve", f_memset(Zv[:, :, 0:PAD], 0.0), writes=["zpad"])

        def WSR(s):
            return [("ws", s, 0), ("ws", s, 1), ("ws", s, 2)]

        def load_w(dst, srcs):
            s = next_slot()
            d = dst(WS[s])
            n = len(srcs)
            for i, (pf, src) in enumerate(srcs):
                wr = [("ws", s, i)] + ([("ws", s, q) for q in range(n, 3)] if i == n - 1 else [])
                S.op("pool", f_dma(pf(d), src), writes=wr, dma=f"ws{s}")
            return s, d

        def norm_stats(ti, dst=None):
            c0, n = TILES[ti]
            pr, pn = NB.next()
            for k in range(KC):
                sr, sa = SQ.next()
                S.op("act", f_act(sa[:, :n], H[:, k, c0:c0 + n], AF.Square), reads=[("h", k, ti)], writes=[sr])
                S.op("pe", f_mm(pn[:, :n], ONES[:], sa[:, :n], k == 0, k == KC - 1), reads=[sr, "ones"], writes=[pr])
            if dst is None:
                rr, ra = RS.next()
                ra = ra[:, :n]
            else:
                rr, ra = dst
            S.op("act", f_act(ra, pn[:, :n], AF.Sqrt, bias=pc("eps_rms"), scale=1.0), reads=[pr, "par"], writes=[rr])
            S.op("dve", f_recip(ra, ra), reads=[rr], writes=[rr])
            return rr, ra

        def norm_to_U(gname):
            for ti in range(NT):
                c0, n = TILES[ti]
                rr, ra = norm_stats(ti)
                for k in range(KC):
                    S.op("dve", f_stt(Uv[:, k, c0:c0 + n], H[:, k, c0:c0 + n], pc(gname, k), ra, ALU.mult, ALU.mult),
                         reads=[("h", k, ti), rr, "par"], writes=[("u", k, ti)])

        def dump(stage):
            if debug:
                for k in range(KC):
                    S.op("sp", f_dma(dbg[stage, k * 128:(k + 1) * 128, :], H[:, k, :]), reads=HALL(k), dma=f"dbg{k}")

        par_i = [0]

        def parity():
            par_i[0] ^= 1
            return par_i[0]

        def proj_add(wname, row0, nk, src_reg, src_ap, bias_name=None, scale_name=None):
            for mb in range(2):
                s, d = load_w(lambda ws: ws[:, 0:nk * 512].rearrange("p (kc n) -> p kc n", kc=nk),
                              [(lambda dd: dd, W[wname][row0:row0 + nk * 128, mb * 512:(mb + 1) * 512].rearrange("(kc p) n -> p kc n", p=128))])
                for mi in range(4):
                    m = mb * 4 + mi
                    for ti in range(NT):
                        c0, n = TILES[ti]
                        p = parity()
                        pd = PS[f"D{p}"]
                        for k in range(nk):
                            S.op("pe", f_mm(pd[:, :n], d[:, k, mi * 128:(mi + 1) * 128], src_ap(k, c0, n), k == 0, k == nk - 1),
                                 reads=WSR(s) + [(src_reg, k, ti)], writes=[("ps", f"D{p}")])
                        hh = H[:, m, c0:c0 + n]
                        if bias_name is not None:
                            fn = f_stt(hh, pd[:, :n], pc(bias_name, m), hh, ALU.add, ALU.add)
                        elif scale_name is not None:
                            fn = f_stt(hh, pd[:, :n], pc(scale_name, m), hh, ALU.mult, ALU.add)
                        else:
                            fn = f_tt(hh, hh, pd[:, :n], ALU.add)
                        S.op("dve", fn, reads=[("ps", f"D{p}"), ("h", m, ti), "par"], writes=[("h", m, ti)])

        def ffn(L):
            norm_to_U(f"ln2_{L}")
            gu = W[f"ffn{L}_w_gu"].rearrange("(kc p) (two f) -> p kc two f", p=128, two=2)
            for (j0, j1) in FGROUPS:
                for jp in range(j0, j1, 2):
                    nch = min(2, j1 - jp)
                    ncol = nch * 128
                    s, d = load_w(lambda ws: ws[:, 0:KC * 2 * ncol].rearrange("p (kc two f) -> p kc two f", kc=KC, two=2),
                                  [(lambda dd, q=q: dd[:, :, q, :], gu[:, :, q, jp * 128:jp * 128 + ncol]) for q in range(2)])
                    for jj in range(nch):
                        j = jp + jj
                        for ti in range(NT):
                            c0, n = TILES[ti]
                            p = parity()
                            pg, pu = PS[f"G{p}"], PS[f"U{p}"]
                            for k in range(KC):
                                S.op("pe", f_mm(pg[:, :n], d[:, k, 0, jj * 128:(jj + 1) * 128], Uv[:, k, c0:c0 + n], k == 0, k == KC - 1),
                                     reads=WSR(s) + [("u", k, ti)], writes=[("ps", f"G{p}")])
                            for k in range(KC):
                                S.op("pe", f_mm(pu[:, :n], d[:, k, 1, jj * 128:(jj + 1) * 128], Uv[:, k, c0:c0 + n], k == 0, k == KC - 1),
                                     reads=WSR(s) + [("u", k, ti)], writes=[("ps", f"U{p}")])
                            tr, ta = TMP.next()
                            S.op("act", f_act(ta[:, :n], pg[:, :n], AF.Silu), reads=[("ps", f"G{p}")], writes=[tr])
                            S.op("dve", f_tt(zt(j - j0, c0, n), ta[:, :n], pu[:, :n], ALU.mult),
                                 reads=[tr, ("ps", f"U{p}")], writes=[("z", j - j0, ti)])
                proj_add(f"ffn{L}_w_down", j0 * 128, j1 - j0, "z", zt)

        def mixer_a(L):
            norm_to_U(f"ln1_{L}")
            w_in = W[f"a{L}_w_in"].rearrange("(kc p) (three f) -> p kc three f", p=128, three=3)
            cname = f"a{L}_conv"
            for j in range(KC):
                s, d = load_w(lambda ws: ws[:, 0:KC * 3 * 128].rearrange("p (kc three f) -> p kc three f", kc=KC, three=3),
                              [(lambda dd, q=q: dd[:, :, q, :], w_in[:, :, q, j * 128:(j + 1) * 128]) for q in range(3)])
                prev = None
                for ti in range(NT):
                    c0, n = TILES[ti]
                    p = parity()
                    banks = [PS[f"D{p}"], PS[f"G{p}"], PS[f"U{p}"]]
                    bnames = [("ps", f"D{p}"), ("ps", f"G{p}"), ("ps", f"U{p}")]
                    for which in (1, 2, 0):
                        for k in range(KC):
                            S.op("pe", f_mm(banks[which][:, :n], d[:, k, which, :], Uv[:, k, c0:c0 + n], k == 0, k == KC - 1),
                                 reads=WSR(s) + [("u", k, ti)], writes=[bnames[which]])
                    cr, ca = TMP.next()
                    S.op("act", f_act(ca[:, :n], banks[1][:, :n], AF.Copy), reads=[bnames[1]], writes=[cr])
                    vr, va = TMP.next()
                    if prev is None:
                        S.op("dve", f_memset(va[:, 0:2], 0.0), writes=[vr])
                    else:
                        pr_, pa_, pn_ = prev
                        S.op("dve", f_copy(va[:, 0:2], pa_[:, pn_:pn_ + 2]), reads=[pr_], writes=[vr])
                    S.op("dve", f_tt(va[:, 2:2 + n], ca[:, :n], banks[2][:, :n], ALU.mult), reads=[cr, bnames[2], vr], writes=[vr])
                    if ti == 0:
                        S.op("dve", f_tt(va[:, 2:2 + n], va[:, 2:2 + n], pc("mask", 0, HALO), ALU.mult), reads=[vr, "par"], writes=[vr])
                    ar, aa = TMP.next()
                    S.op("act", f_act(aa[:, :n], va[:, 0:n], AF.Identity, scale=pc(cname, 3 * j + 0)), reads=[vr, "par"], writes=[ar])
                    S.op("dve", f_stt(aa[:, :n], va[:, 1:1 + n], pc(cname, 3 * j + 1), aa[:, :n], ALU.mult, ALU.add), reads=[vr, ar, "par"], writes=[ar])
                    S.op("dve", f_stt(aa[:, :n], va[:, 2:2 + n], pc(cname, 3 * j + 2), aa[:, :n], ALU.mult, ALU.add), reads=[vr, ar, "par"], writes=[ar])
                    S.op("dve", f_tt(zt(j, c0, n), aa[:, :n], banks[0][:, :n], ALU.mult), reads=[ar, bnames[0]], writes=[("z", j, ti)])
                    prev = (vr, va, n)
            proj_add(f"a{L}_w_out", 0, KC, "z", zt)

        def mixer_b():
            ubn = ["ub0", "ub1", "ub2", "ub3"]
            S.op("dve", f_memset(SCR[:, 0:1], 0.0), writes=UALL + ubn + ["scr"])
            for ti in range(NT):
                c0, n = TILES[ti]
                norm_stats(ti, dst=("ub0", UB[0][:, c0:c0 + n]))
            grp = W["b1_w_grp"].rearrange("g (kc p) n -> p g kc n", p=128)
            s, d = load_w(lambda ws: ws[:, 0:2048].rearrange("p (g kc n) -> p g kc n", g=4, kc=2),
                          [(lambda dd, q=q: dd[:, :, q, :], grp[:, :, q, :]) for q in range(2)])
            uc, sA, sB = UB[1], UB[2], UB[3]
            for c in range(KC):
                g = c // 2
                w = POOL_W[g]
                S.op("dve", f_stt(uc[:, :], H[:, c, :], pc("ln1_1", c), UB[0][:, :], ALU.mult, ALU.mult),
                     reads=HALL(c) + ["ub0", "par"], writes=["ub1"])
                S.op("dve", f_tt(uc[:, 0:HALO], uc[:, 0:HALO], pc("mask", 0, HALO), ALU.mult), reads=["ub1", "par"], writes=["ub1"])
                cur, curn = uc, "ub1"
                dd = 1
                tgt = [(sA, "ub2"), (sB, "ub3")]
                step = 0
                while dd < w:
                    nx, nxn = tgt[step % 2]
                    S.op("dve", f_tt(nx[:, dd:T], cur[:, dd:T], cur[:, 0:T - dd], ALU.add), reads=[curn], writes=[nxn])
                    S.op("act", f_act(nx[:, 0:dd], cur[:, 0:dd], AF.Copy), reads=[curn], writes=[nxn])
                    cur, curn = nx, nxn
                    dd *= 2
                    step += 1
                zall = [("z", c, ti) for ti in range(NT)]
                S.op("dve", f_stt(Zv[:, c, PAD:PAD + T], cur[:, :], 1.0 / w, uc[:, :], ALU.mult, ALU.subtract),
                     reads=[curn, "ub1"], writes=zall)
                tr, ta = TMP.next()
                o, _ = _PL["tab"]
                S.op("dve", f_tt(ta[:, 0:NTAB], cur[:, 0:NTAB], PAR[:, o + g * NTAB:o + (g + 1) * NTAB], ALU.mult), reads=[curn, "par"], writes=[tr])
                S.op("dve", f_tt(Zv[:, c, PAD:PAD + NTAB], ta[:, 0:NTAB], uc[:, 0:NTAB], ALU.subtract), reads=[tr, "ub1"],
                     writes=[("z", c, 0), ("z", c, 1)])
            for g in range(4):
                for mo in range(2):
                    m = 2 * g + mo
                    for ti in range(NT):
                        c0, n = TILES[ti]
                        p = parity()
                        pd = PS[f"D{p}"]
                        for kc in range(2):
                            S.op("pe", f_mm(pd[:, :n], d[:, g, kc, mo * 128:(mo + 1) * 128], zt(2 * g + kc, c0, n), kc == 0, kc == 1),
                                 reads=WSR(s) + [("z", 2 * g + kc, ti)], writes=[("ps", f"D{p}")])
                        hh = H[:, m, c0:c0 + n]
                        S.op("dve", f_stt(hh, pd[:, :n], pc("b1_scale", m), hh, ALU.mult, ALU.add),
                             reads=[("ps", f"D{p}"), ("h", m, ti), "par"], writes=[("h", m, ti)])
            S.op("dve", f_memset(SCR[:, 0:1], 0.0), writes=UALL + ubn + ["scr"])

        def mixer_c():
            norm_to_U("ln1_2")
            pw1 = W["c2_w_pw1"].rearrange("(kc p) (two f) -> p kc two f", p=128, two=2)
            for jp in range(0, KC, 2):
                s, d = load_w(lambda ws: ws[:, 0:KC * 2 * 256].rearrange("p (kc two f) -> p kc two f", kc=KC, two=2),
                              [(lambda dd, q=q: dd[:, :, q, :], pw1[:, :, q, jp * 128:jp * 128 + 256]) for q in range(2)])
                for jj in range(2):
                    j = jp + jj
                    for ti in range(NT):
                        c0, n = TILES[ti]
                        p = parity()
                        pa, pg = PS[f"G{p}"], PS[f"U{p}"]
                        for k in range(KC):
                            S.op("pe", f_mm(pa[:, :n], d[:, k, 0, jj * 128:(jj + 1) * 128], Uv[:, k, c0:c0 + n], k == 0, k == KC - 1),
                                 reads=WSR(s) + [("u", k, ti)], writes=[("ps", f"G{p}")])
                        for k in range(KC):
                            S.op("pe", f_mm(pg[:, :n], d[:, k, 1, jj * 128:(jj + 1) * 128], Uv[:, k, c0:c0 + n], k == 0, k == KC - 1),
                                 reads=WSR(s) + [("u", k, ti)], writes=[("ps", f"U{p}")])
                        tr, ta = TMP.next()
                        S.op("act", f_act(ta[:, :n], pg[:, :n], AF.Sigmoid, bias=pc("c2_b_pw1", 8 + j), scale=1.0),
                             reads=[("ps", f"U{p}"), "par"], writes=[tr])
                        S.op("dve", f_stt(zt(j, c0, n), pa[:, :n], pc("c2_b_pw1", j), ta[:, :n], ALU.add, ALU.mult),
                             reads=[tr, ("ps", f"G{p}"), "par"], writes=[("z", j, ti)])
                        if ti == 0:
                            S.op("dve", f_tt(zt(j, c0, n), zt(j, c0, n), pc("mask", 0, HALO), ALU.mult), reads=[("z", j, ti), "par"],
                                 writes=[("z", j, ti)])
            dwo, _ = _PL["c2_dw"]
            for ti in range(NT):
                c0, n = TILES[ti]
                pmr, pm = ("ps", "N0"), PS["N0"]
                pqr, pq = ("ps", "N1"), PS["N1"]
                for c in range(KC):
                    dr, da = DG.next()
                    S.op("dve", f_tt(da[:], IDB[:].unsqueeze(1).to_broadcast([128, CONFK, 128]),
                                     PAR[:, dwo + c * CONFK:dwo + (c + 1) * CONFK].unsqueeze(2).to_broadcast([128, CONFK, 128]), ALU.mult),
                         reads=["idb", "par"], writes=[dr])
                    p = parity()
                    pd = PS[f"D{p}"]
                    zr = [("z", c, ti)] + ([("z", c, ti - 1)] if ti > 0 else ["zpad"])
                    for k in range(CONFK):
                        a0 = PAD + c0 - (CONFK - 1) + k
                        S.op("pe", f_mm(pd[:, :n], da[:, k, :], Zv[:, c, a0:a0 + n], k == 0, k == CONFK - 1),
                             reads=[dr] + zr, writes=[("ps", f"D{p}")])
                    ys = WSR(2 + c // 4)
                    ya = YT[c // 4][:, (c % 4) * 512:(c % 4) * 512 + n]
                    bb = pc("c2_b_dw", c)
                    S.op("act", f_act(ya, pd[:, :n], AF.Identity, bias=bb, scale=1.0), reads=[("ps", f"D{p}"), "par"], writes=ys + [("yt", c)])
                    s1r, s1 = SQ.next()
                    S.op("act", f_act(s1[:, :n], pd[:, :n], AF.Identity, bias=bb, scale=1.0), reads=[("ps", f"D{p}"), "par"], writes=[s1r])
                    S.op("pe", f_mm(pm[:, :n], ONES[:], s1[:, :n], c == 0, c == KC - 1), reads=[s1r, "ones"], writes=[pmr])
                    s2r, s2 = SQ.next()
                    S.op("act", f_act(s2[:, :n], pd[:, :n], AF.Square, bias=bb, scale=1.0), reads=[("ps", f"D{p}"), "par"], writes=[s2r])
                    S.op("pe", f_mm(pq[:, :n], ONES[:], s2[:, :n], c == 0, c == KC - 1), reads=[s2r, "ones"], writes=[pqr])
                mur, mu = RS.next()
                mu = mu[:, :n]
                S.op("dve", f_copy(mu, pm[:, :n]), reads=[pmr], writes=[mur])
                rr, ra = RS.next()
                ra = ra[:, :n]
                S.op("dve", f_tt(ra, mu, mu, ALU.mult), reads=[mur], writes=[rr])
                S.op("dve", f_tt(ra, pq[:, :n], ra, ALU.subtract), reads=[pqr, rr], writes=[rr])
                S.op("act", f_act(ra, ra, AF.Sqrt, bias=pc("eps_ln"), scale=1.0), reads=[rr, "par"], writes=[rr])
                S.op("dve", f_recip(ra, ra), reads=[rr], writes=[rr])
                for c in range(KC):
                    ys = WSR(2 + c // 4)
                    ya = YT[c // 4][:, (c % 4) * 512:(c % 4) * 512 + n]
                    tr, ta = TMP.next()
                    S.op("dve", f_tt(ta[:, :n], ya, mu, ALU.subtract), reads=[("yt", c), mur] + ys, writes=[tr])
                    S.op("dve", f_tt(ta[:, :n], ta[:, :n], ra, ALU.mult), reads=[tr, rr], writes=[tr])
                    S.op("act", f_act(Uv[:, c, c0:c0 + n], ta[:, :n], AF.Silu, bias=pc("c2_ln_b", c), scale=pc("c2_ln_g", c)),
                         reads=[tr, "par"], writes=[("u", c, ti)])
            proj_add("c2_w_pw2", 0, KC, "u", lambda k, c0, n: Uv[:, k, c0:c0 + n], bias_name="c2_b_pw2")

        def final():
            for ti in range(1, NT):
                c0, n = TILES[ti]
                rr, ra = norm_stats(ti)
                for k in range(KC):
                    tr, ta = TMP.next()
                    S.op("dve", f_stt(ta[:, :n], H[:, k, c0:c0 + n], pc("ln_f", k), ra, ALU.mult, ALU.mult),
                         reads=[("h", k, ti), rr, "par"], writes=[tr])
                    S.op("sp", f_dma(yT[k * 128:(k + 1) * 128, c0 - HALO:c0 - HALO + n], ta[:, :n]), reads=[tr], dma=f"o{tr[1]}")

        mixers = [lambda: mixer_a(0), mixer_b, mixer_c, lambda: mixer_a(3)]
        for L in range(nlayers):
            S.phase = L
            mixers[L]()
            dump(2 * L)
            ffn(L)
            dump(2 * L + 1)
        S.phase = 4
        final()
        S.emit()
        nc._sched_counts = S.counts
        nc._n_ops = len(S.ops)
    return nc


def make_in_maps(inp):
    x = np.asarray(inp["x"], np.float32)
    wts = {n: np.ascontiguousarray(np.asarray(inp[n], np.float32)) for n, _ in WSHAPES}
    in_maps = []
    for core in range(8):
        b, half = core // 2, core % 2
        xs = np.zeros((T, D), np.float32)
        if half == 0:
            xs[HALO:] = x[b, 0:TCORE]
        else:
            xs[:] = x[b, TCORE - HALO:2 * TCORE]
        m = {"xT": np.ascontiguousarray(xs.T), "par": build_params(inp, half)}
        m.update(wts)
        in_maps.append(m)
    return in_maps


def kernel(**inp):
    in_maps = make_in_maps(inp)
    nc = build_program()
    res = run_bass_kernel_spmd(nc, in_maps, core_ids=list(range(8)))
    out = np.empty((BATCH, SEQ, D), np.float32)
    for core in range(8):
        b, half = core // 2, core % 2
        out[b, half * TCORE:(half + 1) * TCORE] = np.asarray(res.results[core]["yT"]).T
    return out
```

```python
import numpy as np
from contextlib import ExitStack
import concourse.bass as bass
import concourse.mybir as mybir
from concourse.bass_utils import run_bass_kernel_spmd

F32 = mybir.dt.float32
BF16 = mybir.dt.bfloat16
ALU = mybir.AluOpType
AF = mybir.ActivationFunctionType

D = 1024
KC = 8
FF = 2816
FC = 22
SEQ = 4096
BATCH = 4
TCORE = 2048
HALO = 64
T = TCORE + HALO
PAD = 32
ZT = T + PAD
TILES = [(0, 424), (424, 424), (848, 424), (1272, 424), (1696, 416)]
NT = len(TILES)
POOL_W = (2, 4, 8, 16)
NTAB = HALO + 16
CONFK = 31
FGROUPS = [(0, 8), (8, 15), (15, 22)]

_PL = {}
_off = 0
for _n, _c in ([(f"ln1_{i}", 8) for i in range(4)] + [(f"ln2_{i}", 8) for i in range(4)] + [("ln_f", 8)] +
               [("a0_conv", 24), ("a3_conv", 24), ("b1_scale", 8), ("c2_b_pw1", 16), ("c2_dw", 8 * CONFK),
                ("c2_b_dw", 8), ("c2_ln_g", 8), ("c2_ln_b", 8), ("c2_b_pw2", 8), ("eps_rms", 1), ("eps_ln", 1),
                ("mask", HALO), ("tab", 4 * NTAB), ("ident", 128)]):
    _PL[_n] = (_off, _c)
    _off += _c
NPAR = _off

WSHAPES = []
for _i in (0, 3):
    WSHAPES += [(f"a{_i}_w_in", [D, 3 * D]), (f"a{_i}_w_out", [D, D])]
WSHAPES += [("b1_w_grp", [4, 256, 256]), ("c2_w_pw1", [D, 2 * D]), ("c2_w_pw2", [D, D])]
for _i in range(4):
    WSHAPES += [(f"ffn{_i}_w_gu", [D, 2 * FF]), (f"ffn{_i}_w_down", [FF, D])]


def _vec(v, n):
    return np.ascontiguousarray(np.asarray(v, np.float32).reshape(n, 128).T)


def build_params(inp, half):
    P = np.zeros((128, NPAR), np.float32)

    def put(name, arr):
        o, c = _PL[name]
        P[:, o:o + c] = np.asarray(arr, np.float32).reshape(128, c)

    for i in range(4):
        put(f"ln1_{i}", _vec(inp[f"ln1_{i}"], 8))
        put(f"ln2_{i}", _vec(inp[f"ln2_{i}"], 8))
    put("ln_f", _vec(inp["ln_f"], 8))
    for nm in ("a0_conv", "a3_conv"):
        w = np.asarray(inp[nm], np.float32)
        put(nm, w.reshape(3, 8, 128).transpose(2, 1, 0))
    put("b1_scale", _vec(inp["b1_scale"], 8))
    put("c2_b_pw1", _vec(inp["c2_b_pw1"], 16))
    put("c2_dw", np.asarray(inp["c2_dw"], np.float32).reshape(CONFK, 8, 128).transpose(2, 1, 0))
    for nm in ("c2_b_dw", "c2_ln_g", "c2_ln_b", "c2_b_pw2"):
        put(nm, _vec(inp[nm], 8))
    put("eps_rms", np.full((128, 1), 1e-6, np.float32))
    put("eps_ln", np.full((128, 1), 1e-5, np.float32))
    put("mask", np.full((128, HALO), 1.0 if half == 1 else 0.0, np.float32))
    tab = np.zeros((4, NTAB), np.float32)
    for g, w in enumerate(POOL_W):
        for j in range(NTAB):
            t_abs = half * TCORE + j - HALO
            cnt = min(t_abs + 1, w) if t_abs >= 0 else w
            tab[g, j] = 1.0 / cnt
    put("tab", np.broadcast_to(tab.reshape(1, 4 * NTAB), (128, 4 * NTAB)))
    put("ident", np.eye(128, dtype=np.float32))
    return P


class Sched:
    def __init__(self, nc):
        self.nc = nc
        self.ops = []
        self.last_w = {}
        self.readers = {}
        self.phase = 0

    def op(self, eng, fn, reads=(), writes=(), dma=None):
        i = len(self.ops)
        deps = set()
        for r in reads:
            if r in self.last_w:
                deps.add(self.last_w[r])
        for r in writes:
            if r in self.last_w:
                deps.add(self.last_w[r])
            rd = self.readers.get(r)
            if rd:
                deps.update(rd.values())
        key = eng if dma is None else ("dma", i)
        for r in reads:
            self.readers.setdefault(r, {})[key] = i
        for r in writes:
            self.last_w[r] = i
            self.readers[r] = {}
        deps.discard(i)
        if eng == "pe":
            deps = {d for d in deps if not (self.ops[d]["eng"] == "pe" and self.ops[d]["dma"] is None)}
        self.ops.append(dict(eng=eng, fn=fn, deps=deps, dma=dma, need=False, phase=self.phase))
        return i

    def emit(self):
        nc = self.nc
        ops = self.ops
        for o in ops:
            for d in o["deps"]:
                ops[d]["need"] = True
            if o["dma"] is not None:
                o["need"] = True
        last_on = {}
        for i, o in enumerate(ops):
            last_on[o["eng"]] = i
        for i in last_on.values():
            ops[i]["need"] = True
        cnt = {}
        for o in ops:
            if not o["need"]:
                o["ev"] = None
                continue
            if o["dma"] is not None:
                s = "d_" + o["dma"]
                cnt[s] = cnt.get(s, 0) + 16
            else:
                s = "e%d_%s" % (o["phase"], o["eng"])
                cnt[s] = cnt.get(s, 0) + 1
            o["ev"] = (s, cnt[s])
        self.counts = dict(cnt)

        def merge(a, b):
            for s_, v_ in b.items():
                if v_ > a.get(s_, 0):
                    a[s_] = v_

        K = [None] * len(ops)
        issue_known = {}
        last_c = {}
        nwait = 0
        for i, o in enumerate(ops):
            E = o["eng"]
            tmp = issue_known.setdefault(E, {})
            waits = {}
            for d in sorted(o["deps"], reverse=True):
                s_, v_ = ops[d]["ev"]
                if tmp.get(s_, 0) >= v_:
                    continue
                if v_ > waits.get(s_, 0):
                    waits[s_] = v_
                merge(tmp, K[d])
            o["waits"] = waits
            nwait += len(waits)
            k = dict(tmp)
            if o["dma"] is None:
                if E in last_c:
                    merge(k, K[last_c[E]])
                last_c[E] = i
            if o["ev"] is not None:
                if o["ev"][1] > k.get(o["ev"][0], 0):
                    k[o["ev"][0]] = o["ev"][1]
            K[i] = k
        self.nwait = nwait
        with ExitStack() as es:
            sems = {s: es.enter_context(nc.semaphore(s)) for s in sorted(cnt)}
            block = es.enter_context(nc.Block())

            def make(engname):
                def body(eng):
                    for o in ops:
                        if o["eng"] != engname:
                            continue
                        for s, v in o["waits"].items():
                            eng.wait_ge(sems[s], v)
                        ins = o["fn"](eng)
                        if o["ev"] is not None:
                            ins.then_inc(sems[o["ev"][0]], 16 if o["dma"] is not None else 1)
                    if engname == "sp":
                        known = issue_known.get("sp", {})
                        for s, v in cnt.items():
                            if v > known.get(s, 0):
                                eng.wait_ge(sems[s], v)
                return body

            block.tensor(make("pe"))
            block.scalar(make("act"))
            block.vector(make("dve"))
            block.gpsimd(make("pool"))
            block.sync(make("sp"))


class Rot:
    def __init__(self, name, bufs):
        self.name, self.bufs, self.i = name, bufs, 0

    def next(self):
        j = self.i % len(self.bufs)
        self.i += 1
        return (self.name, j), self.bufs[j]


def f_mm(out, lhsT, rhs, start, stop):
    return lambda e: e.matmul(out, lhsT=lhsT, rhs=rhs, start=start, stop=stop)


def f_act(out, in_, func, bias=None, scale=None):
    kw = {}
    if bias is not None:
        kw["bias"] = bias
    if scale is not None:
        kw["scale"] = scale
    return lambda e: e.activation(out=out, in_=in_, func=func, **kw)


def f_tt(out, in0, in1, op):
    return lambda e: e.tensor_tensor(out=out, in0=in0, in1=in1, op=op)


def f_stt(out, in0, scalar, in1, op0, op1):
    return lambda e: e.scalar_tensor_tensor(out=out, in0=in0, scalar=scalar, in1=in1, op0=op0, op1=op1)


def f_copy(out, in_):
    return lambda e: e.tensor_copy(out=out, in_=in_)


def f_recip(out, in_):
    return lambda e: e.reciprocal(out=out, in_=in_)


def f_memset(ap, v):
    return lambda e: e.memset(ap, v)


def f_dma(out, in_):
    return lambda e: e.dma_start(out=out, in_=in_)


def build_program(debug=False, nlayers=4):
    nc = bass.Bass("TRN2", target_bir_lowering=False)
    xT = nc.dram_tensor("xT", [D, T], F32, kind="ExternalInput").ap()
    par = nc.dram_tensor("par", [128, NPAR], F32, kind="ExternalInput").ap()
    W = {n: nc.dram_tensor(n, s, F32, kind="ExternalInput").ap() for n, s in WSHAPES}
    yT = nc.dram_tensor("yT", [D, TCORE], F32, kind="ExternalOutput").ap()
    dbg = nc.dram_tensor("dbg", [8, D, T], F32, kind="ExternalOutput").ap() if debug else None

    with ExitStack() as es:
        def sb(name, shape, dt):
            return es.enter_context(nc.sbuf_tensor(name, shape, dt))

        H = sb("H", [128, KC, T], F32)
        Uf = sb("U", [128, KC * T], BF16)
        Zf = sb("Z", [128, KC * ZT], BF16)
        WS = [sb(f"ws{i}", [128, 4096], BF16) for i in range(4)]
        PAR = sb("PAR", [128, NPAR], F32)
        ONES = sb("ones", [128, 128], BF16)
        IDB = sb("idb", [128, 128], BF16)
        TMPb = [sb(f"tmp{i}", [128, 544], F32) for i in range(5)]
        SQb = [sb(f"sq{i}", [128, 512], BF16) for i in range(4)]
        RSb = [sb(f"rs{i}", [128, 512], F32) for i in range(3)]
        DGb = [sb(f"dg{i}", [128, CONFK, 128], BF16) for i in range(2)]
        SCR = sb("scr", [128, 8], F32)
        PS = {n: es.enter_context(nc.psum_tensor("ps" + n, [128, 512], F32))
              for n in ("G0", "G1", "U0", "U1", "D0", "D1", "N0", "N1")}

        Uv = Uf[:].rearrange("p (k t) -> p k t", k=KC)
        Zv = Zf[:].rearrange("p (k t) -> p k t", k=KC)
        UB = [Uf[:, i * 2 * T:(i + 1) * 2 * T].bitcast(F32) for i in range(4)]
        YT = [WS[2][:].bitcast(F32), WS[3][:].bitcast(F32)]

        def zt(k, c0, n):
            return Zv[:, k, PAD + c0:PAD + c0 + n]

        def ut(k, c0, n):
            return Uv[:, k, c0:c0 + n]

        def pc(name, c=0, n=1):
            o, _ = _PL[name]
            return PAR[:, o + c:o + c + n]

        S = Sched(nc)
        TMP = Rot("tmp", TMPb)
        SQ = Rot("sq", SQb)
        RS = Rot("rs", RSb)
        DG = Rot("dg", DGb)
        NB = Rot("psN", [PS["N0"], PS["N1"]])
        slot_i = [0]

        def next_slot():
            s = slot_i[0] % 4
            slot_i[0] += 1
            return s

        HALL = lambda k: [("h", k, ti) for ti in range(NT)]
        UALL = [("u", k, ti) for k in range(KC) for ti in range(NT)]

        S.op("sp", f_dma(PAR[:], par), writes=["par"], dma="par")
        xv = xT.rearrange("(k p) t -> p k t", p=128)
        for ti in range(NT):
            c0, n = TILES[ti]
            S.op("sp", f_dma(H[:, :, c0:c0 + n], xv[:, :, c0:c0 + n]), writes=[("h", k, ti) for k in range(KC)], dma=f"x{ti}")
        S.op("dve", f_memset(ONES[:], 1.0 / D), writes=["ones"])
        S.op("dve", f_copy(IDB[:], pc("ident", 0, 128)), reads=["par"], writes=["idb"])
        S.op("dve", f_memset(Zv[:, :, 0:PAD], 0.0), writes=["zpad"])

        def WSR(s):
            return [("ws", s, 0), ("ws", s, 1), ("ws", s, 2)]

        def load_w(dst, srcs):
            s = next_slot()
            d = dst(WS[s])
            n = len(srcs)
            for i, (pf, src) in enumerate(srcs):
                wr = [("ws", s, i)] + ([("ws", s, q) for q in range(n, 3)] if i == n - 1 else [])
                S.op("pool", f_dma(pf(d), src), writes=wr, dma=f"ws{s}")
            return s, d

        def recip(rr, ra):
            S.op("dve", f_recip(ra, ra), reads=[rr], writes=[rr])

        def norm_stats(ti, dst=None):
            c0, n = TILES[ti]
            pr, pn = NB.next()
            for k in range(KC):
                sr, sa = SQ.next()
                S.op("act", f_act(sa[:, :n], H[:, k, c0:c0 + n], AF.Square), reads=[("h", k, ti)], writes=[sr])
                S.op("pe", f_mm(pn[:, :n], ONES[:], sa[:, :n], k == 0, k == KC - 1), reads=[sr, "ones"], writes=[pr])
            if dst is None:
                rr, ra = RS.next()
                ra = ra[:, :n]
            else:
                rr, ra = dst
            S.op("act", f_act(ra, pn[:, :n], AF.Sqrt, bias=pc("eps_rms"), scale=1.0), reads=[pr, "par"], writes=[rr])
            recip(rr, ra)
            return rr, ra

        def norm_cb_U(gname):
            def cb(ti):
                c0, n = TILES[ti]
                rr, ra = norm_stats(ti)
                for k in range(KC):
                    S.op("dve", f_stt(ut(k, c0, n), H[:, k, c0:c0 + n], pc(gname, k), ra, ALU.mult, ALU.mult),
                         reads=[("h", k, ti), rr, "par"], writes=[("u", k, ti)])
            return cb

        def dump(stage):
            if debug:
                for k in range(KC):
                    S.op("sp", f_dma(dbg[stage, k * 128:(k + 1) * 128, :], H[:, k, :]), reads=HALL(k), dma=f"dbg{k}")

        par_i = [0]

        def parity():
            par_i[0] ^= 1
            return par_i[0]

        def proj_add(wname, row0, nk, src_reg, src_ap, bias_name=None, next_cb=None):
            def load(mb):
                return load_w(lambda ws: ws[:, 0:nk * 512].rearrange("p (kc n) -> p kc n", kc=nk),
                              [(lambda dd: dd, W[wname][row0:row0 + nk * 128, mb * 512:(mb + 1) * 512].rearrange("(kc p) n -> p kc n", p=128))])

            def group(s, d, mb, mi, ti):
                m = mb * 4 + mi
                c0, n = TILES[ti]
                p = parity()
                pd = PS[f"D{p}"]
                for k in range(nk):
                    S.op("pe", f_mm(pd[:, :n], d[:, k, mi * 128:(mi + 1) * 128], src_ap(k, c0, n), k == 0, k == nk - 1),
                         reads=WSR(s) + [(src_reg, k, ti)], writes=[("ps", f"D{p}")])
                hh = H[:, m, c0:c0 + n]
                if bias_name is not None:
                    fn = f_stt(hh, pd[:, :n], pc(bias_name, m), hh, ALU.add, ALU.add)
                else:
                    fn = f_tt(hh, hh, pd[:, :n], ALU.add)
                S.op("dve", fn, reads=[("ps", f"D{p}"), ("h", m, ti), "par"], writes=[("h", m, ti)])

            if next_cb is None:
                for mb in range(2):
                    s, d = load(mb)
                    for mi in range(4):
                        for ti in range(NT):
                            group(s, d, mb, mi, ti)
            else:
                blocks = [load(0), load(1)]
                for ti in range(NT):
                    for mb in range(2):
                        for mi in range(4):
                            group(blocks[mb][0], blocks[mb][1], mb, mi, ti)
                    if ti >= 1:
                        next_cb(ti - 1)
                next_cb(NT - 1)

        def ffn(L, next_cb):
            gu = W[f"ffn{L}_w_gu"].rearrange("(kc p) (two f) -> p kc two f", p=128, two=2)
            for gi, (j0, j1) in enumerate(FGROUPS):
                for jp in range(j0, j1, 2):
                    nch = min(2, j1 - jp)
                    ncol = nch * 128
                    s, d = load_w(lambda ws: ws[:, 0:KC * 2 * ncol].rearrange("p (kc two f) -> p kc two f", kc=KC, two=2),
                                  [(lambda dd, q=q: dd[:, :, q, :], gu[:, :, q, jp * 128:jp * 128 + ncol]) for q in range(2)])
                    for jj in range(nch):
                        j = jp + jj
                        for ti in range(NT):
                            c0, n = TILES[ti]
                            p = parity()
                            pg, pu = PS[f"G{p}"], PS[f"U{p}"]
                            for k in range(KC):
                                S.op("pe", f_mm(pg[:, :n], d[:, k, 0, jj * 128:(jj + 1) * 128], ut(k, c0, n), k == 0, k == KC - 1),
                                     reads=WSR(s) + [("u", k, ti)], writes=[("ps", f"G{p}")])
                            for k in range(KC):
                                S.op("pe", f_mm(pu[:, :n], d[:, k, 1, jj * 128:(jj + 1) * 128], ut(k, c0, n), k == 0, k == KC - 1),
                                     reads=WSR(s) + [("u", k, ti)], writes=[("ps", f"U{p}")])
                            tr, ta = TMP.next()
                            S.op("act", f_act(ta[:, :n], pg[:, :n], AF.Silu), reads=[("ps", f"G{p}")], writes=[tr])
                            S.op("dve", f_tt(zt(j - j0, c0, n), ta[:, :n], pu[:, :n], ALU.mult),
                                 reads=[tr, ("ps", f"U{p}")], writes=[("z", j - j0, ti)])
                proj_add(f"ffn{L}_w_down", j0 * 128, j1 - j0, "z", zt,
                         next_cb=(next_cb if gi == len(FGROUPS) - 1 else None))

        def mixer_a(L, next_cb):
            w_in = W[f"a{L}_w_in"].rearrange("(kc p) (three f) -> p kc three f", p=128, three=3)
            cname = f"a{L}_conv"
            for j in range(KC):
                s, d = load_w(lambda ws: ws[:, 0:KC * 3 * 128].rearrange("p (kc three f) -> p kc three f", kc=KC, three=3),
                              [(lambda dd, q=q: dd[:, :, q, :], w_in[:, :, q, j * 128:(j + 1) * 128]) for q in range(3)])
                prev = None
                for ti in range(NT):
                    c0, n = TILES[ti]
                    p = parity()
                    banks = [PS[f"D{p}"], PS[f"G{p}"], PS[f"U{p}"]]
                    bnames = [("ps", f"D{p}"), ("ps", f"G{p}"), ("ps", f"U{p}")]
                    for which in (1, 2, 0):
                        for k in range(KC):
                            S.op("pe", f_mm(banks[which][:, :n], d[:, k, which, :], ut(k, c0, n), k == 0, k == KC - 1),
                                 reads=WSR(s) + [("u", k, ti)], writes=[bnames[which]])
                    cr, ca = TMP.next()
                    S.op("act", f_act(ca[:, :n], banks[1][:, :n], AF.Copy), reads=[bnames[1]], writes=[cr])
                    vr, va = TMP.next()
                    if prev is None:
                        S.op("dve", f_memset(va[:, 0:2], 0.0), writes=[vr])
                    else:
                        pr_, pa_, pn_ = prev
                        S.op("dve", f_copy(va[:, 0:2], pa_[:, pn_:pn_ + 2]), reads=[pr_], writes=[vr])
                    S.op("dve", f_tt(va[:, 2:2 + n], ca[:, :n], banks[2][:, :n], ALU.mult), reads=[cr, bnames[2], vr], writes=[vr])
                    if ti == 0:
                        S.op("dve", f_tt(va[:, 2:2 + HALO], va[:, 2:2 + HALO], pc("mask", 0, HALO), ALU.mult), reads=[vr, "par"], writes=[vr])
                    ar, aa = TMP.next()
                    S.op("act", f_act(aa[:, :n], va[:, 0:n], AF.Identity, scale=pc(cname, 3 * j + 0)), reads=[vr, "par"], writes=[ar])
                    S.op("dve", f_stt(aa[:, :n], va[:, 1:1 + n], pc(cname, 3 * j + 1), aa[:, :n], ALU.mult, ALU.add), reads=[vr, ar, "par"], writes=[ar])
                    S.op("dve", f_stt(aa[:, :n], va[:, 2:2 + n], pc(cname, 3 * j + 2), aa[:, :n], ALU.mult, ALU.add), reads=[vr, ar, "par"], writes=[ar])
                    S.op("dve", f_tt(zt(j, c0, n), aa[:, :n], banks[0][:, :n], ALU.mult), reads=[ar, bnames[0]], writes=[("z", j, ti)])
                    prev = (vr, va, n)
            proj_add(f"a{L}_w_out", 0, KC, "z", zt, next_cb=next_cb)

        ubn = ["ub0", "ub1", "ub2", "ub3"]

        def norm_cb_b(ti):
            if ti == 0:
                S.op("dve", f_memset(SCR[:, 0:1], 0.0), writes=UALL + ubn + ["scr"])
            c0, n = TILES[ti]
            norm_stats(ti, dst=("ub0", UB[0][:, c0:c0 + n]))

        def mixer_b(next_cb):
            grp = W["b1_w_grp"].rearrange("g (kc p) n -> p g kc n", p=128)
            s, d = load_w(lambda ws: ws[:, 0:2048].rearrange("p (g kc n) -> p g kc n", g=4, kc=2),
                          [(lambda dd, q=q: dd[:, :, q, :], grp[:, :, q, :]) for q in range(2)])
            uc, sA, sB = UB[1], UB[2], UB[3]
            for c in range(KC):
                g = c // 2
                w = POOL_W[g]
                S.op("dve", f_stt(uc[:, :], H[:, c, :], pc("ln1_1", c), UB[0][:, :], ALU.mult, ALU.mult),
                     reads=HALL(c) + ["ub0", "par"], writes=["ub1"])
                S.op("dve", f_tt(uc[:, 0:HALO], uc[:, 0:HALO], pc("mask", 0, HALO), ALU.mult), reads=["ub1", "par"], writes=["ub1"])
                cur, curn = uc, "ub1"
                dd = 1
                tgt = [(sA, "ub2"), (sB, "ub3")]
                step = 0
                while dd < w:
                    nx, nxn = tgt[step % 2]
                    S.op("dve", f_tt(nx[:, dd:T], cur[:, dd:T], cur[:, 0:T - dd], ALU.add), reads=[curn], writes=[nxn])
                    S.op("act", f_act(nx[:, 0:dd], cur[:, 0:dd], AF.Copy), reads=[curn], writes=[nxn])
                    cur, curn = nx, nxn
                    dd *= 2
                    step += 1
                zall = [("z", c, ti) for ti in range(NT)]
                S.op("dve", f_stt(Zv[:, c, PAD:PAD + T], cur[:, :], 1.0 / w, uc[:, :], ALU.mult, ALU.subtract),
                     reads=[curn, "ub1"], writes=zall)
                tr, ta = TMP.next()
                o, _ = _PL["tab"]
                S.op("dve", f_tt(ta[:, 0:NTAB], cur[:, 0:NTAB], PAR[:, o + g * NTAB:o + (g + 1) * NTAB], ALU.mult), reads=[curn, "par"], writes=[tr])
                S.op("dve", f_tt(Zv[:, c, PAD:PAD + NTAB], ta[:, 0:NTAB], uc[:, 0:NTAB], ALU.subtract), reads=[tr, "ub1"],
                     writes=[("z", c, 0)])
            S.op("dve", f_memset(SCR[:, 0:1], 0.0), writes=UALL + ubn + ["scr"])
            for g in range(4):
                last = (g == 3 and next_cb is not None)
                if last:
                    order = [(ti, mo) for ti in range(NT) for mo in range(2)]
                else:
                    order = [(ti, mo) for mo in range(2) for ti in range(NT)]
                for (ti, mo) in order:
                    m = 2 * g + mo
                    c0, n = TILES[ti]
                    p = parity()
                    pd = PS[f"D{p}"]
                    for kc in range(2):
                        S.op("pe", f_mm(pd[:, :n], d[:, g, kc, mo * 128:(mo + 1) * 128], zt(2 * g + kc, c0, n), kc == 0, kc == 1),
                             reads=WSR(s) + [("z", 2 * g + kc, ti)], writes=[("ps", f"D{p}")])
                    hh = H[:, m, c0:c0 + n]
                    S.op("dve", f_stt(hh, pd[:, :n], pc("b1_scale", m), hh, ALU.mult, ALU.add),
                         reads=[("ps", f"D{p}"), ("h", m, ti), "par"], writes=[("h", m, ti)])
                    if last and mo == 1 and ti >= 1:
                        next_cb(ti - 1)
                if last:
                    next_cb(NT - 1)

        def mixer_c(next_cb):
            pw1 = W["c2_w_pw1"].rearrange("(kc p) (two f) -> p kc two f", p=128, two=2)
            for jp in range(0, KC, 2):
                s, d = load_w(lambda ws: ws[:, 0:KC * 2 * 256].rearrange("p (kc two f) -> p kc two f", kc=KC, two=2),
                              [(lambda dd, q=q: dd[:, :, q, :], pw1[:, :, q, jp * 128:jp * 128 + 256]) for q in range(2)])
                for jj in range(2):
                    j = jp + jj
                    for ti in range(NT):
                        c0, n = TILES[ti]
                        p = parity()
                        pa, pg = PS[f"G{p}"], PS[f"U{p}"]
                        for k in range(KC):
                            S.op("pe", f_mm(pa[:, :n], d[:, k, 0, jj * 128:(jj + 1) * 128], ut(k, c0, n), k == 0, k == KC - 1),
                                 reads=WSR(s) + [("u", k, ti)], writes=[("ps", f"G{p}")])
                        for k in range(KC):
                            S.op("pe", f_mm(pg[:, :n], d[:, k, 1, jj * 128:(jj + 1) * 128], ut(k, c0, n), k == 0, k == KC - 1),
                                 reads=WSR(s) + [("u", k, ti)], writes=[("ps", f"U{p}")])
                        tr, ta = TMP.next()
                        S.op("act", f_act(ta[:, :n], pg[:, :n], AF.Sigmoid, bias=pc("c2_b_pw1", 8 + j), scale=1.0),
                             reads=[("ps", f"U{p}"), "par"], writes=[tr])
                        S.op("dve", f_stt(zt(j, c0, n), pa[:, :n], pc("c2_b_pw1", j), ta[:, :n], ALU.add, ALU.mult),
                             reads=[tr, ("ps", f"G{p}"), "par"], writes=[("z", j, ti)])
                        if ti == 0:
                            S.op("dve", f_tt(zt(j, 0, HALO), zt(j, 0, HALO), pc("mask", 0, HALO), ALU.mult), reads=[("z", j, ti), "par"],
                                 writes=[("z", j, ti)])
            dwo, _ = _PL["c2_dw"]
            pmr, pm = ("ps", "N0"), PS["N0"]
            pqr, pq = ("ps", "N1"), PS["N1"]
            steps = [(ti, c) for ti in range(NT) for c in range(KC)]
            diags = {}

            def build_diag(si):
                c = steps[si][1]
                dr, da = DG.next()
                S.op("dve", f_tt(da[:], IDB[:].unsqueeze(1).to_broadcast([128, CONFK, 128]),
                                 PAR[:, dwo + c * CONFK:dwo + (c + 1) * CONFK].unsqueeze(2).to_broadcast([128, CONFK, 128]), ALU.mult),
                     reads=["idb", "par"], writes=[dr])
                diags[si] = (dr, da)

            def ytile(c, n):
                return WSR(2 + c // 4), YT[c // 4][:, (c % 4) * 512:(c % 4) * 512 + n]

            def stats_mm(pend):
                ti, c, s1r, s1, s2r, s2 = pend
                n = TILES[ti][1]
                S.op("pe", f_mm(pm[:, :n], ONES[:], s1[:, :n], c == 0, c == KC - 1), reads=[s1r, "ones"], writes=[pmr])
                S.op("pe", f_mm(pq[:, :n], ONES[:], s2[:, :n], c == 0, c == KC - 1), reads=[s2r, "ones"], writes=[pqr])

            def ln_pre(ti):
                c0, n = TILES[ti]
                mur, mu = RS.next()
                mu = mu[:, :n]
                S.op("dve", f_copy(mu, pm[:, :n]), reads=[pmr], writes=[mur])
                rr, ra = RS.next()
                ra = ra[:, :n]
                S.op("dve", f_tt(ra, mu, mu, ALU.mult), reads=[mur], writes=[rr])
                S.op("dve", f_tt(ra, pq[:, :n], ra, ALU.subtract), reads=[pqr, rr], writes=[rr])
                S.op("act", f_act(ra, ra, AF.Sqrt, bias=pc("eps_ln"), scale=1.0), reads=[rr, "par"], writes=[rr])
                recip(rr, ra)
                return (ti, mur, mu, rr, ra)

            def ln_chunk(st, c):
                ti, mur, mu, rr, ra = st
                c0, n = TILES[ti]
                ys, ya = ytile(c, n)
                tr, ta = TMP.next()
                S.op("dve", f_tt(ta[:, :n], ya, mu, ALU.subtract), reads=[("yt", c), mur] + ys, writes=[tr])
                S.op("dve", f_tt(ta[:, :n], ta[:, :n], ra, ALU.mult), reads=[tr, rr], writes=[tr])
                S.op("act", f_act(ut(c, c0, n), ta[:, :n], AF.Silu, bias=pc("c2_ln_b", c), scale=pc("c2_ln_g", c)),
                     reads=[tr, "par"], writes=[("u", c, ti)])

            build_diag(0)
            pend = None
            lnst = None
            for si, (ti, c) in enumerate(steps):
                c0, n = TILES[ti]
                if si + 1 < len(steps):
                    build_diag(si + 1)
                dr, da = diags[si]
                p = parity()
                pd = PS[f"D{p}"]
                zr = [("z", c, ti)] + ([("z", c, ti - 1)] if ti > 0 else ["zpad"])
                for k in range(CONFK):
                    a0 = PAD + c0 - (CONFK - 1) + k
                    S.op("pe", f_mm(pd[:, :n], da[:, k, :], Zv[:, c, a0:a0 + n], k == 0, k == CONFK - 1),
                         reads=[dr] + zr, writes=[("ps", f"D{p}")])
                if pend is not None:
                    stats_mm(pend)
                    if pend[1] == KC - 1:
                        lnst = ln_pre(pend[0])
                if lnst is not None:
                    ln_chunk(lnst, c)
                ys, ya = ytile(c, n)
                bb = pc("c2_b_dw", c)
                S.op("act", f_act(ya, pd[:, :n], AF.Identity, bias=bb, scale=1.0), reads=[("ps", f"D{p}"), "par"], writes=ys + [("yt", c)])
                s1r, s1 = SQ.next()
                S.op("act", f_act(s1[:, :n], pd[:, :n], AF.Identity, bias=bb, scale=1.0), reads=[("ps", f"D{p}"), "par"], writes=[s1r])
                s2r, s2 = SQ.next()
                S.op("act", f_act(s2[:, :n], pd[:, :n], AF.Square, bias=bb, scale=1.0), reads=[("ps", f"D{p}"), "par"], writes=[s2r])
                pend = (ti, c, s1r, s1, s2r, s2)
            stats_mm(pend)
            lnst = ln_pre(pend[0])
            for c in range(KC):
                ln_chunk(lnst, c)
            slot_i[0] = 0
            proj_add("c2_w_pw2", 0, KC, "u", ut, bias_name="c2_b_pw2", next_cb=next_cb)

        def final_cb(ti):
            c0, n = TILES[ti]
            lo = max(c0, HALO)
            rr, ra = norm_stats(ti)
            for k in range(KC):
                tr, ta = TMP.next()
                S.op("dve", f_stt(ta[:, :n], H[:, k, c0:c0 + n], pc("ln_f", k), ra, ALU.mult, ALU.mult),
                     reads=[("h", k, ti), rr, "par"], writes=[tr])
                S.op("sp", f_dma(yT[k * 128:(k + 1) * 128, lo - HALO:c0 + n - HALO], ta[:, lo - c0:n]), reads=[tr], dma=f"o{tr[1]}")

        phases = []
        mix_norm = [norm_cb_U("ln1_0"), norm_cb_b, norm_cb_U("ln1_2"), norm_cb_U("ln1_3")]
        mix_body = [lambda cb: mixer_a(0, cb), mixer_b, mixer_c, lambda cb: mixer_a(3, cb)]
        for L in range(nlayers):
            phases.append((L, mix_norm[L], mix_body[L]))
            phases.append((L, norm_cb_U(f"ln2_{L}"), (lambda cb, L=L: ffn(L, cb))))
        if nlayers == 4:
            phases.append((4, final_cb, None))
        for ti in range(NT):
            phases[0][1](ti)
        for i, (ph, ncb, body) in enumerate(phases):
            S.phase = ph
            if body is None:
                break
            nxt = phases[i + 1][1] if i + 1 < len(phases) else None
            body(nxt)
            dump(i)
        S.emit()
        nc._sched_counts = S.counts
        nc._n_ops = len(S.ops)
        nc._nwait = S.nwait
    return nc


def make_in_maps(inp):
    x = np.asarray(inp["x"], np.float32)
    wts = {n: np.ascontiguousarray(np.asarray(inp[n], np.float32)) for n, _ in WSHAPES}
    in_maps = []
    for core in range(8):
        b, half = core // 2, core % 2
        xs = np.zeros((T, D), np.float32)
        if half == 0:
            xs[HALO:] = x[b, 0:TCORE]
        else:
            xs[:] = x[b, TCORE - HALO:2 * TCORE]
        m = {"xT": np.ascontiguousarray(xs.T), "par": build_params(inp, half)}
        m.update(wts)
        in_maps.append(m)
    return in_maps


def kernel(**inp):
    in_maps = make_in_maps(inp)
    nc = build_program()
    res = run_bass_kernel_spmd(nc, in_maps, core_ids=list(range(8)))
    out = np.empty((BATCH, SEQ, D), np.float32)
    for core in range(8):
        b, half = core // 2, core % 2
        out[b, half * TCORE:(half + 1) * TCORE] = np.asarray(res.results[core]["yT"]).T
    return out
```

```python
import numpy as np
from contextlib import ExitStack
import concourse.bass as bass
import concourse.mybir as mybir
from concourse.bass_utils import run_bass_kernel_spmd

F32 = mybir.dt.float32
BF16 = mybir.dt.bfloat16
ALU = mybir.AluOpType
AF = mybir.ActivationFunctionType

D = 1024
KC = 8
FF = 2816
FC = 22
SEQ = 4096
BATCH = 4
TCORE = 2048
HALO = 64
T = TCORE + HALO
PAD = 32
ZT = T + PAD
TILES = [(0, 424), (424, 424), (848, 424), (1272, 424), (1696, 416)]
NT = len(TILES)
POOL_W = (2, 4, 8, 16)
NTAB = HALO + 16
CONFK = 31
FGROUPS = [(0, 8), (8, 15), (15, 22)]

_PL = {}
_off = 0
for _n, _c in ([(f"ln1_{i}", 8) for i in range(4)] + [(f"ln2_{i}", 8) for i in range(4)] + [("ln_f", 8)] +
               [("a0_conv", 24), ("a3_conv", 24), ("b1_scale", 8), ("c2_b_pw1", 16), ("c2_dw", 8 * CONFK),
                ("c2_b_dw", 8), ("c2_ln_g", 8), ("c2_ln_b", 8), ("c2_b_pw2", 8), ("eps_rms", 1), ("eps_ln", 1),
                ("mask", HALO), ("tab", 4 * NTAB), ("ident", 128)]):
    _PL[_n] = (_off, _c)
    _off += _c
NPAR = _off

WSHAPES = []
for _i in (0, 3):
    WSHAPES += [(f"a{_i}_w_in", [D, 3 * D]), (f"a{_i}_w_out", [D, D])]
WSHAPES += [("b1_w_grp", [4, 256, 256]), ("c2_w_pw1", [D, 2 * D]), ("c2_w_pw2", [D, D])]
for _i in range(4):
    WSHAPES += [(f"ffn{_i}_w_gu", [D, 2 * FF]), (f"ffn{_i}_w_down", [FF, D])]


def _vec(v, n):
    return np.ascontiguousarray(np.asarray(v, np.float32).reshape(n, 128).T)


def build_params(inp, half):
    P = np.zeros((128, NPAR), np.float32)

    def put(name, arr):
        o, c = _PL[name]
        P[:, o:o + c] = np.asarray(arr, np.float32).reshape(128, c)

    for i in range(4):
        put(f"ln1_{i}", _vec(inp[f"ln1_{i}"], 8))
        put(f"ln2_{i}", _vec(inp[f"ln2_{i}"], 8))
    put("ln_f", _vec(inp["ln_f"], 8))
    for nm in ("a0_conv", "a3_conv"):
        w = np.asarray(inp[nm], np.float32)
        put(nm, w.reshape(3, 8, 128).transpose(2, 1, 0))
    put("b1_scale", _vec(inp["b1_scale"], 8))
    put("c2_b_pw1", _vec(inp["c2_b_pw1"], 16))
    put("c2_dw", np.asarray(inp["c2_dw"], np.float32).reshape(CONFK, 8, 128).transpose(2, 1, 0))
    for nm in ("c2_b_dw", "c2_ln_g", "c2_ln_b", "c2_b_pw2"):
        put(nm, _vec(inp[nm], 8))
    put("eps_rms", np.full((128, 1), 1e-6, np.float32))
    put("eps_ln", np.full((128, 1), 1e-5, np.float32))
    put("mask", np.full((128, HALO), 1.0 if half == 1 else 0.0, np.float32))
    tab = np.zeros((4, NTAB), np.float32)
    for g, w in enumerate(POOL_W):
        for j in range(NTAB):
            t_abs = half * TCORE + j - HALO
            cnt = min(t_abs + 1, w) if t_abs >= 0 else w
            tab[g, j] = 1.0 / cnt
    put("tab", np.broadcast_to(tab.reshape(1, 4 * NTAB), (128, 4 * NTAB)))
    put("ident", np.eye(128, dtype=np.float32))
    return P


class Sched:
    def __init__(self, nc):
        self.nc = nc
        self.ops = []
        self.last_w = {}
        self.readers = {}
        self.phase = 0

    def op(self, eng, fn, reads=(), writes=(), dma=None):
        i = len(self.ops)
        deps = set()
        for r in reads:
            if r in self.last_w:
                deps.add(self.last_w[r])
        for r in writes:
            if r in self.last_w:
                deps.add(self.last_w[r])
            rd = self.readers.get(r)
            if rd:
                deps.update(rd.values())
        key = eng if dma is None else ("dma", i)
        for r in reads:
            self.readers.setdefault(r, {})[key] = i
        for r in writes:
            self.last_w[r] = i
            self.readers[r] = {}
        deps.discard(i)
        if eng == "pe":
            deps = {d for d in deps if not (self.ops[d]["eng"] == "pe" and self.ops[d]["dma"] is None)}
        self.ops.append(dict(eng=eng, fn=fn, deps=deps, dma=dma, need=False, phase=self.phase))
        return i

    def emit(self):
        nc = self.nc
        ops = self.ops
        for o in ops:
            for d in o["deps"]:
                ops[d]["need"] = True
            if o["dma"] is not None:
                o["need"] = True
        last_on = {}
        for i, o in enumerate(ops):
            last_on[o["eng"]] = i
        for i in last_on.values():
            ops[i]["need"] = True
        cnt = {}
        for o in ops:
            if not o["need"]:
                o["ev"] = None
                continue
            if o["dma"] is not None:
                s = "d_" + o["dma"]
                cnt[s] = cnt.get(s, 0) + 16
            else:
                s = "e%d_%s" % (o["phase"], o["eng"])
                cnt[s] = cnt.get(s, 0) + 1
            o["ev"] = (s, cnt[s])
        self.counts = dict(cnt)

        def merge(a, b):
            for s_, v_ in b.items():
                if v_ > a.get(s_, 0):
                    a[s_] = v_

        K = [None] * len(ops)
        issue_known = {}
        last_c = {}
        nwait = 0
        for i, o in enumerate(ops):
            E = o["eng"]
            tmp = issue_known.setdefault(E, {})
            waits = {}
            for d in sorted(o["deps"], reverse=True):
                s_, v_ = ops[d]["ev"]
                if tmp.get(s_, 0) >= v_:
                    continue
                if v_ > waits.get(s_, 0):
                    waits[s_] = v_
                merge(tmp, K[d])
            o["waits"] = waits
            nwait += len(waits)
            k = dict(tmp)
            if o["dma"] is None:
                if E in last_c:
                    merge(k, K[last_c[E]])
                last_c[E] = i
            if o["ev"] is not None:
                if o["ev"][1] > k.get(o["ev"][0], 0):
                    k[o["ev"][0]] = o["ev"][1]
            K[i] = k
        self.nwait = nwait
        with ExitStack() as es:
            sems = {s: es.enter_context(nc.semaphore(s)) for s in sorted(cnt)}
            block = es.enter_context(nc.Block())

            def make(engname):
                def body(eng):
                    for o in ops:
                        if o["eng"] != engname:
                            continue
                        for s, v in o["waits"].items():
                            eng.wait_ge(sems[s], v)
                        ins = o["fn"](eng)
                        if o["ev"] is not None:
                            ins.then_inc(sems[o["ev"][0]], 16 if o["dma"] is not None else 1)
                    if engname == "sp":
                        known = issue_known.get("sp", {})
                        for s, v in cnt.items():
                            if v > known.get(s, 0):
                                eng.wait_ge(sems[s], v)
                return body

            block.tensor(make("pe"))
            block.scalar(make("act"))
            block.vector(make("dve"))
            block.gpsimd(make("pool"))
            block.sync(make("sp"))


class Rot:
    def __init__(self, name, bufs):
        self.name, self.bufs, self.i = name, bufs, 0

    def next(self):
        j = self.i % len(self.bufs)
        self.i += 1
        return (self.name, j), self.bufs[j]


def f_mm(out, lhsT, rhs, start, stop):
    return lambda e: e.matmul(out, lhsT=lhsT, rhs=rhs, start=start, stop=stop)


def f_act(out, in_, func, bias=None, scale=None):
    kw = {}
    if bias is not None:
        kw["bias"] = bias
    if scale is not None:
        kw["scale"] = scale
    return lambda e: e.activation(out=out, in_=in_, func=func, **kw)


def f_tt(out, in0, in1, op):
    return lambda e: e.tensor_tensor(out=out, in0=in0, in1=in1, op=op)


def f_stt(out, in0, scalar, in1, op0, op1):
    return lambda e: e.scalar_tensor_tensor(out=out, in0=in0, scalar=scalar, in1=in1, op0=op0, op1=op1)


def f_copy(out, in_):
    return lambda e: e.tensor_copy(out=out, in_=in_)


def f_recip(out, in_):
    return lambda e: e.reciprocal(out=out, in_=in_)


def f_memset(ap, v):
    return lambda e: e.memset(ap, v)


def f_dma(out, in_):
    return lambda e: e.dma_start(out=out, in_=in_)


def build_program(debug=False, nlayers=4):
    nc = bass.Bass("TRN2", target_bir_lowering=False)
    xT = nc.dram_tensor("xT", [D, T], F32, kind="ExternalInput").ap()
    par = nc.dram_tensor("par", [128, NPAR], F32, kind="ExternalInput").ap()
    W = {n: nc.dram_tensor(n, s, F32, kind="ExternalInput").ap() for n, s in WSHAPES}
    yT = nc.dram_tensor("yT", [D, TCORE], F32, kind="ExternalOutput").ap()
    dbg = nc.dram_tensor("dbg", [8, D, T], F32, kind="ExternalOutput").ap() if debug else None

    with ExitStack() as es:
        def sb(name, shape, dt):
            return es.enter_context(nc.sbuf_tensor(name, shape, dt))

        H = sb("H", [128, KC, T], F32)
        Uf = sb("U", [128, KC * T], BF16)
        Zf = sb("Z", [128, KC * ZT], BF16)
        WS = [sb(f"ws{i}", [128, 4096], BF16) for i in range(4)]
        PAR = sb("PAR", [128, NPAR], F32)
        ONES = sb("ones", [128, 128], BF16)
        IDB = sb("idb", [128, 128], BF16)
        TMPb = [sb(f"tmp{i}", [128, 544], F32) for i in range(5)]
        SQb = [sb(f"sq{i}", [128, 512], BF16) for i in range(4)]
        RSb = [sb(f"rs{i}", [128, 512], F32) for i in range(3)]
        DGb = [sb(f"dg{i}", [128, CONFK, 128], BF16) for i in range(2)]
        SCR = sb("scr", [128, 8], F32)
        PS = {n: es.enter_context(nc.psum_tensor("ps" + n, [128, 512], F32))
              for n in ("G0", "G1", "U0", "U1", "D0", "D1", "N0", "N1")}

        Uv = Uf[:].rearrange("p (k t) -> p k t", k=KC)
        Zv = Zf[:].rearrange("p (k t) -> p k t", k=KC)
        UB = [Uf[:, i * 2 * T:(i + 1) * 2 * T].bitcast(F32) for i in range(4)]
        YT = [WS[2][:].bitcast(F32), WS[3][:].bitcast(F32)]

        def zt(k, c0, n):
            return Zv[:, k, PAD + c0:PAD + c0 + n]

        def ut(k, c0, n):
            return Uv[:, k, c0:c0 + n]

        def pc(name, c=0, n=1):
            o, _ = _PL[name]
            return PAR[:, o + c:o + c + n]

        S = Sched(nc)
        TMP = Rot("tmp", TMPb)
        SQ = Rot("sq", SQb)
        RS = Rot("rs", RSb)
        DG = Rot("dg", DGb)
        NB = Rot("psN", [PS["N0"], PS["N1"]])
        slot_i = [0]

        def next_slot():
            s = slot_i[0] % 4
            slot_i[0] += 1
            return s

        HALL = lambda k: [("h", k, ti) for ti in range(NT)]
        UALL = [("u", k, ti) for k in range(KC) for ti in range(NT)]

        S.op("sp", f_dma(PAR[:], par), writes=["par"], dma="par")
        xv = xT.rearrange("(k p) t -> p k t", p=128)
        for ti in range(NT):
            c0, n = TILES[ti]
            S.op("sp", f_dma(H[:, :, c0:c0 + n], xv[:, :, c0:c0 + n]), writes=[("h", k, ti) for k in range(KC)], dma=f"x{ti}")
        S.op("dve", f_memset(ONES[:], 1.0 / D), writes=["ones"])
        S.op("dve", f_copy(IDB[:], pc("ident", 0, 128)), reads=["par"], writes=["idb"])
        S.op("dve", f_memset(Zv[:, :, 0:PAD], 0.0), writes=["zpad"])

        def WSR(s):
            return [("ws", s, 0), ("ws", s, 1), ("ws", s, 2)]

        def load_w(dst, srcs):
            s = next_slot()
            d = dst(WS[s])
            n = len(srcs)
            for i, (pf, src) in enumerate(srcs):
                wr = [("ws", s, i)] + ([("ws", s, q) for q in range(n, 3)] if i == n - 1 else [])
                S.op("pool", f_dma(pf(d), src), writes=wr, dma=f"ws{s}")
            return s, d

        def recip(rr, ra):
            S.op("dve", f_recip(ra, ra), reads=[rr], writes=[rr])

        def norm_stats(ti, dst=None):
            c0, n = TILES[ti]
            pr, pn = NB.next()
            for k in range(KC):
                sr, sa = SQ.next()
                S.op("act", f_act(sa[:, :n], H[:, k, c0:c0 + n], AF.Square), reads=[("h", k, ti)], writes=[sr])
                S.op("pe", f_mm(pn[:, :n], ONES[:], sa[:, :n], k == 0, k == KC - 1), reads=[sr, "ones"], writes=[pr])
            if dst is None:
                rr, ra = RS.next()
                ra = ra[:, :n]
            else:
                rr, ra = dst
            S.op("act", f_act(ra, pn[:, :n], AF.Sqrt, bias=pc("eps_rms"), scale=1.0), reads=[pr, "par"], writes=[rr])
            return rr, ra, (lambda: recip(rr, ra))

        def norm_cb_U(gname):
            def cb(ti):
                c0, n = TILES[ti]
                rr, ra, th = norm_stats(ti)
                ths = [th]
                for k in range(KC):
                    ths.append(lambda k=k: S.op(
                        "dve", f_stt(ut(k, c0, n), H[:, k, c0:c0 + n], pc(gname, k), ra, ALU.mult, ALU.mult),
                        reads=[("h", k, ti), rr, "par"], writes=[("u", k, ti)]))
                return ths
            return cb

        def run_all(ths):
            for th in ths:
                th()

        def dump(stage):
            if debug:
                for k in range(KC):
                    S.op("sp", f_dma(dbg[stage, k * 128:(k + 1) * 128, :], H[:, k, :]), reads=HALL(k), dma=f"dbg{k}")

        par_i = [0]

        def parity():
            par_i[0] ^= 1
            return par_i[0]

        def proj_add(wname, row0, nk, src_reg, src_ap, bias_name=None, next_cb=None):
            def load(mb):
                return load_w(lambda ws: ws[:, 0:nk * 512].rearrange("p (kc n) -> p kc n", kc=nk),
                              [(lambda dd: dd, W[wname][row0:row0 + nk * 128, mb * 512:(mb + 1) * 512].rearrange("(kc p) n -> p kc n", p=128))])

            def group(s, d, mb, mi, ti):
                m = mb * 4 + mi
                c0, n = TILES[ti]
                p = parity()
                pd = PS[f"D{p}"]
                for k in range(nk):
                    S.op("pe", f_mm(pd[:, :n], d[:, k, mi * 128:(mi + 1) * 128], src_ap(k, c0, n), k == 0, k == nk - 1),
                         reads=WSR(s) + [(src_reg, k, ti)], writes=[("ps", f"D{p}")])
                hh = H[:, m, c0:c0 + n]
                if bias_name is not None:
                    fn = f_stt(hh, pd[:, :n], pc(bias_name, m), hh, ALU.add, ALU.add)
                else:
                    fn = f_tt(hh, hh, pd[:, :n], ALU.add)
                S.op("dve", fn, reads=[("ps", f"D{p}"), ("h", m, ti), "par"], writes=[("h", m, ti)])

            if next_cb is None:
                for mb in range(2):
                    s, d = load(mb)
                    for mi in range(4):
                        for ti in range(NT):
                            group(s, d, mb, mi, ti)
            else:
                blocks = [load(0), load(1)]
                deferred = []
                for ti in range(NT):
                    for mb in range(2):
                        for mi in range(4):
                            group(blocks[mb][0], blocks[mb][1], mb, mi, ti)
                            for _ in range(2):
                                if deferred:
                                    deferred.pop(0)()
                    run_all(deferred)
                    deferred = []
                    if ti >= 1:
                        deferred = list(next_cb(ti - 1))
                run_all(deferred)
                run_all(next_cb(NT - 1))

        def ffn(L, next_cb):
            gu = W[f"ffn{L}_w_gu"].rearrange("(kc p) (two f) -> p kc two f", p=128, two=2)
            for gi, (j0, j1) in enumerate(FGROUPS):
                for jp in range(j0, j1, 2):
                    nch = min(2, j1 - jp)
                    ncol = nch * 128
                    s, d = load_w(lambda ws: ws[:, 0:KC * 2 * ncol].rearrange("p (kc two f) -> p kc two f", kc=KC, two=2),
                                  [(lambda dd, q=q: dd[:, :, q, :], gu[:, :, q, jp * 128:jp * 128 + ncol]) for q in range(2)])
                    for jj in range(nch):
                        j = jp + jj
                        for ti in range(NT):
                            c0, n = TILES[ti]
                            p = parity()
                            pg, pu = PS[f"G{p}"], PS[f"U{p}"]
                            for k in range(KC):
                                S.op("pe", f_mm(pg[:, :n], d[:, k, 0, jj * 128:(jj + 1) * 128], ut(k, c0, n), k == 0, k == KC - 1),
                                     reads=WSR(s) + [("u", k, ti)], writes=[("ps", f"G{p}")])
                            for k in range(KC):
                                S.op("pe", f_mm(pu[:, :n], d[:, k, 1, jj * 128:(jj + 1) * 128], ut(k, c0, n), k == 0, k == KC - 1),
                                     reads=WSR(s) + [("u", k, ti)], writes=[("ps", f"U{p}")])
                            tr, ta = TMP.next()
                            S.op("act", f_act(ta[:, :n], pg[:, :n], AF.Silu), reads=[("ps", f"G{p}")], writes=[tr])
                            S.op("dve", f_tt(zt(j - j0, c0, n), ta[:, :n], pu[:, :n], ALU.mult),
                                 reads=[tr, ("ps", f"U{p}")], writes=[("z", j - j0, ti)])
                proj_add(f"ffn{L}_w_down", j0 * 128, j1 - j0, "z", zt,
                         next_cb=(next_cb if gi == len(FGROUPS) - 1 else None))

        def mixer_a(L, next_cb):
            w_in = W[f"a{L}_w_in"].rearrange("(kc p) (three f) -> p kc three f", p=128, three=3)
            cname = f"a{L}_conv"
            for j in range(KC):
                s, d = load_w(lambda ws: ws[:, 0:KC * 3 * 128].rearrange("p (kc three f) -> p kc three f", kc=KC, three=3),
                              [(lambda dd, q=q: dd[:, :, q, :], w_in[:, :, q, j * 128:(j + 1) * 128]) for q in range(3)])
                prev = None
                for ti in range(NT):
                    c0, n = TILES[ti]
                    p = parity()
                    banks = [PS[f"D{p}"], PS[f"G{p}"], PS[f"U{p}"]]
                    bnames = [("ps", f"D{p}"), ("ps", f"G{p}"), ("ps", f"U{p}")]
                    for which in (1, 2, 0):
                        for k in range(KC):
                            S.op("pe", f_mm(banks[which][:, :n], d[:, k, which, :], ut(k, c0, n), k == 0, k == KC - 1),
                                 reads=WSR(s) + [("u", k, ti)], writes=[bnames[which]])
                    cr, ca = TMP.next()
                    S.op("act", f_act(ca[:, :n], banks[1][:, :n], AF.Copy), reads=[bnames[1]], writes=[cr])
                    vr, va = TMP.next()
                    if prev is None:
                        S.op("dve", f_memset(va[:, 0:2], 0.0), writes=[vr])
                    else:
                        pr_, pa_, pn_ = prev
                        S.op("dve", f_copy(va[:, 0:2], pa_[:, pn_:pn_ + 2]), reads=[pr_], writes=[vr])
                    S.op("dve", f_tt(va[:, 2:2 + n], ca[:, :n], banks[2][:, :n], ALU.mult), reads=[cr, bnames[2], vr], writes=[vr])
                    if ti == 0:
                        S.op("dve", f_tt(va[:, 2:2 + HALO], va[:, 2:2 + HALO], pc("mask", 0, HALO), ALU.mult), reads=[vr, "par"], writes=[vr])
                    ar, aa = TMP.next()
                    S.op("act", f_act(aa[:, :n], va[:, 0:n], AF.Identity, scale=pc(cname, 3 * j + 0)), reads=[vr, "par"], writes=[ar])
                    S.op("dve", f_stt(aa[:, :n], va[:, 1:1 + n], pc(cname, 3 * j + 1), aa[:, :n], ALU.mult, ALU.add), reads=[vr, ar, "par"], writes=[ar])
                    S.op("dve", f_stt(aa[:, :n], va[:, 2:2 + n], pc(cname, 3 * j + 2), aa[:, :n], ALU.mult, ALU.add), reads=[vr, ar, "par"], writes=[ar])
                    S.op("dve", f_tt(zt(j, c0, n), aa[:, :n], banks[0][:, :n], ALU.mult), reads=[ar, bnames[0]], writes=[("z", j, ti)])
                    prev = (vr, va, n)
            proj_add(f"a{L}_w_out", 0, KC, "z", zt, next_cb=next_cb)

        ubn = ["ub0", "ub1", "ub2", "ub3"]

        def norm_cb_b(ti):
            if ti == 0:
                S.op("dve", f_memset(SCR[:, 0:1], 0.0), writes=UALL + ubn + ["scr"])
            c0, n = TILES[ti]
            rr, ra, th = norm_stats(ti, dst=("ub0", UB[0][:, c0:c0 + n]))
            return [th]

        def mixer_b(next_cb):
            grp = W["b1_w_grp"].rearrange("g (kc p) n -> p g kc n", p=128)
            s, d = load_w(lambda ws: ws[:, 0:2048].rearrange("p (g kc n) -> p g kc n", g=4, kc=2),
                          [(lambda dd, q=q: dd[:, :, q, :], grp[:, :, q, :]) for q in range(2)])
            uc, sA, sB = UB[1], UB[2], UB[3]
            for c in range(KC):
                g = c // 2
                w = POOL_W[g]
                S.op("dve", f_stt(uc[:, :], H[:, c, :], pc("ln1_1", c), UB[0][:, :], ALU.mult, ALU.mult),
                     reads=HALL(c) + ["ub0", "par"], writes=["ub1"])
                S.op("dve", f_tt(uc[:, 0:HALO], uc[:, 0:HALO], pc("mask", 0, HALO), ALU.mult), reads=["ub1", "par"], writes=["ub1"])
                cur, curn = uc, "ub1"
                dd = 1
                tgt = [(sA, "ub2"), (sB, "ub3")]
                step = 0
                while dd < w:
                    nx, nxn = tgt[step % 2]
                    S.op("dve", f_tt(nx[:, dd:T], cur[:, dd:T], cur[:, 0:T - dd], ALU.add), reads=[curn], writes=[nxn])
                    S.op("act", f_act(nx[:, 0:dd], cur[:, 0:dd], AF.Copy), reads=[curn], writes=[nxn])
                    cur, curn = nx, nxn
                    dd *= 2
                    step += 1
                zall = [("z", c, ti) for ti in range(NT)]
                S.op("dve", f_stt(Zv[:, c, PAD:PAD + T], cur[:, :], 1.0 / w, uc[:, :], ALU.mult, ALU.subtract),
                     reads=[curn, "ub1"], writes=zall)
                tr, ta = TMP.next()
                o, _ = _PL["tab"]
                S.op("dve", f_tt(ta[:, 0:NTAB], cur[:, 0:NTAB], PAR[:, o + g * NTAB:o + (g + 1) * NTAB], ALU.mult), reads=[curn, "par"], writes=[tr])
                S.op("dve", f_tt(Zv[:, c, PAD:PAD + NTAB], ta[:, 0:NTAB], uc[:, 0:NTAB], ALU.subtract), reads=[tr, "ub1"],
                     writes=[("z", c, 0)])
            S.op("dve", f_memset(SCR[:, 0:1], 0.0), writes=UALL + ubn + ["scr"])
            for g in range(4):
                last = (g == 3 and next_cb is not None)
                if last:
                    order = [(ti, mo) for ti in range(NT) for mo in range(2)]
                else:
                    order = [(ti, mo) for mo in range(2) for ti in range(NT)]
                for (ti, mo) in order:
                    m = 2 * g + mo
                    c0, n = TILES[ti]
                    p = parity()
                    pd = PS[f"D{p}"]
                    for kc in range(2):
                        S.op("pe", f_mm(pd[:, :n], d[:, g, kc, mo * 128:(mo + 1) * 128], zt(2 * g + kc, c0, n), kc == 0, kc == 1),
                             reads=WSR(s) + [("z", 2 * g + kc, ti)], writes=[("ps", f"D{p}")])
                    hh = H[:, m, c0:c0 + n]
                    S.op("dve", f_stt(hh, pd[:, :n], pc("b1_scale", m), hh, ALU.mult, ALU.add),
                         reads=[("ps", f"D{p}"), ("h", m, ti), "par"], writes=[("h", m, ti)])
                    if last and mo == 1 and ti >= 1:
                        run_all(next_cb(ti - 1))
                if last:
                    run_all(next_cb(NT - 1))

        def mixer_c(next_cb):
            pw1 = W["c2_w_pw1"].rearrange("(kc p) (two f) -> p kc two f", p=128, two=2)
            for jp in range(0, KC, 2):
                s, d = load_w(lambda ws: ws[:, 0:KC * 2 * 256].rearrange("p (kc two f) -> p kc two f", kc=KC, two=2),
                              [(lambda dd, q=q: dd[:, :, q, :], pw1[:, :, q, jp * 128:jp * 128 + 256]) for q in range(2)])
                for jj in range(2):
                    j = jp + jj
                    for ti in range(NT):
                        c0, n = TILES[ti]
                        p = parity()
                        pa, pg = PS[f"G{p}"], PS[f"U{p}"]
                        for k in range(KC):
                            S.op("pe", f_mm(pa[:, :n], d[:, k, 0, jj * 128:(jj + 1) * 128], ut(k, c0, n), k == 0, k == KC - 1),
                                 reads=WSR(s) + [("u", k, ti)], writes=[("ps", f"G{p}")])
                        for k in range(KC):
                            S.op("pe", f_mm(pg[:, :n], d[:, k, 1, jj * 128:(jj + 1) * 128], ut(k, c0, n), k == 0, k == KC - 1),
                                 reads=WSR(s) + [("u", k, ti)], writes=[("ps", f"U{p}")])
                        tr, ta = TMP.next()
                        S.op("act", f_act(ta[:, :n], pg[:, :n], AF.Sigmoid, bias=pc("c2_b_pw1", 8 + j), scale=1.0),
                             reads=[("ps", f"U{p}"), "par"], writes=[tr])
                        S.op("dve", f_stt(zt(j, c0, n), pa[:, :n], pc("c2_b_pw1", j), ta[:, :n], ALU.add, ALU.mult),
                             reads=[tr, ("ps", f"G{p}"), "par"], writes=[("z", j, ti)])
                        if ti == 0:
                            S.op("dve", f_tt(zt(j, 0, HALO), zt(j, 0, HALO), pc("mask", 0, HALO), ALU.mult), reads=[("z", j, ti), "par"],
                                 writes=[("z", j, ti)])
            dwo, _ = _PL["c2_dw"]
            pmr, pm = ("ps", "N0"), PS["N0"]
            pqr, pq = ("ps", "N1"), PS["N1"]
            steps = [(ti, c) for ti in range(NT) for c in range(KC)]
            diags = {}

            def build_diag(si):
                c = steps[si][1]
                dr, da = DG.next()
                S.op("dve", f_tt(da[:], IDB[:].unsqueeze(1).to_broadcast([128, CONFK, 128]),
                                 PAR[:, dwo + c * CONFK:dwo + (c + 1) * CONFK].unsqueeze(2).to_broadcast([128, CONFK, 128]), ALU.mult),
                     reads=["idb", "par"], writes=[dr])
                diags[si] = (dr, da)

            def ytile(c, n):
                return WSR(2 + c // 4), YT[c // 4][:, (c % 4) * 512:(c % 4) * 512 + n]

            def stats_mm(pend):
                ti, c, s1r, s1, s2r, s2 = pend
                n = TILES[ti][1]
                S.op("pe", f_mm(pm[:, :n], ONES[:], s1[:, :n], c == 0, c == KC - 1), reads=[s1r, "ones"], writes=[pmr])
                S.op("pe", f_mm(pq[:, :n], ONES[:], s2[:, :n], c == 0, c == KC - 1), reads=[s2r, "ones"], writes=[pqr])

            def ln_pre(ti):
                c0, n = TILES[ti]
                mur, mu = RS.next()
                mu = mu[:, :n]
                S.op("dve", f_copy(mu, pm[:, :n]), reads=[pmr], writes=[mur])
                rr, ra = RS.next()
                ra = ra[:, :n]
                S.op("dve", f_tt(ra, mu, mu, ALU.mult), reads=[mur], writes=[rr])
                S.op("dve", f_tt(ra, pq[:, :n], ra, ALU.subtract), reads=[pqr, rr], writes=[rr])
                S.op("act", f_act(ra, ra, AF.Sqrt, bias=pc("eps_ln"), scale=1.0), reads=[rr, "par"], writes=[rr])
                recip(rr, ra)
                return (ti, mur, mu, rr, ra)

            def ln_chunk(st, c):
                ti, mur, mu, rr, ra = st
                c0, n = TILES[ti]
                ys, ya = ytile(c, n)
                tr, ta = TMP.next()
                S.op("dve", f_tt(ta[:, :n], ya, mu, ALU.subtract), reads=[("yt", c), mur] + ys, writes=[tr])
                S.op("dve", f_tt(ta[:, :n], ta[:, :n], ra, ALU.mult), reads=[tr, rr], writes=[tr])
                S.op("act", f_act(ut(c, c0, n), ta[:, :n], AF.Silu, bias=pc("c2_ln_b", c), scale=pc("c2_ln_g", c)),
                     reads=[tr, "par"], writes=[("u", c, ti)])

            build_diag(0)
            pend = None
            lnst = None
            for si, (ti, c) in enumerate(steps):
                c0, n = TILES[ti]
                if si + 1 < len(steps):
                    build_diag(si + 1)
                dr, da = diags[si]
                p = parity()
                pd = PS[f"D{p}"]
                zr = [("z", c, ti)] + ([("z", c, ti - 1)] if ti > 0 else ["zpad"])
                for k in range(CONFK):
                    a0 = PAD + c0 - (CONFK - 1) + k
                    S.op("pe", f_mm(pd[:, :n], da[:, k, :], Zv[:, c, a0:a0 + n], k == 0, k == CONFK - 1),
                         reads=[dr] + zr, writes=[("ps", f"D{p}")])
                if pend is not None:
                    stats_mm(pend)
                    if pend[1] == KC - 1:
                        lnst = ln_pre(pend[0])
                if lnst is not None:
                    ln_chunk(lnst, c)
                ys, ya = ytile(c, n)
                bb = pc("c2_b_dw", c)
                S.op("act", f_act(ya, pd[:, :n], AF.Identity, bias=bb, scale=1.0), reads=[("ps", f"D{p}"), "par"], writes=ys + [("yt", c)])
                s1r, s1 = SQ.next()
                S.op("act", f_act(s1[:, :n], pd[:, :n], AF.Identity, bias=bb, scale=1.0), reads=[("ps", f"D{p}"), "par"], writes=[s1r])
                s2r, s2 = SQ.next()
                S.op("act", f_act(s2[:, :n], pd[:, :n], AF.Square, bias=bb, scale=1.0), reads=[("ps", f"D{p}"), "par"], writes=[s2r])
                pend = (ti, c, s1r, s1, s2r, s2)
            stats_mm(pend)
            lnst = ln_pre(pend[0])
            for c in range(KC):
                ln_chunk(lnst, c)
            slot_i[0] = 0
            proj_add("c2_w_pw2", 0, KC, "u", ut, bias_name="c2_b_pw2", next_cb=next_cb)

        def final_cb(ti):
            c0, n = TILES[ti]
            lo = max(c0, HALO)
            rr, ra, th = norm_stats(ti)
            ths = [th]

            def out_k(k):
                tr, ta = TMP.next()
                S.op("dve", f_stt(ta[:, :n], H[:, k, c0:c0 + n], pc("ln_f", k), ra, ALU.mult, ALU.mult),
                     reads=[("h", k, ti), rr, "par"], writes=[tr])
                S.op("sp", f_dma(yT[k * 128:(k + 1) * 128, lo - HALO:c0 + n - HALO], ta[:, lo - c0:n]), reads=[tr], dma=f"o{tr[1]}")
            for k in range(KC):
                ths.append(lambda k=k: out_k(k))
            return ths

        phases = []
        mix_norm = [norm_cb_U("ln1_0"), norm_cb_b, norm_cb_U("ln1_2"), norm_cb_U("ln1_3")]
        mix_body = [lambda cb: mixer_a(0, cb), mixer_b, mixer_c, lambda cb: mixer_a(3, cb)]
        for L in range(nlayers):
            phases.append((L, mix_norm[L], mix_body[L]))
            phases.append((L, norm_cb_U(f"ln2_{L}"), (lambda cb, L=L: ffn(L, cb))))
        if nlayers == 4:
            phases.append((4, final_cb, None))
        for ti in range(NT):
            run_all(phases[0][1](ti))
        for i, (ph, ncb, body) in enumerate(phases):
            S.phase = ph
            if body is None:
                break
            nxt = phases[i + 1][1] if i + 1 < len(phases) else None
            body(nxt)
            dump(i)
        S.emit()
        nc._sched_counts = S.counts
        nc._n_ops = len(S.ops)
        nc._nwait = S.nwait
    return nc


def make_in_maps(inp):
    x = np.asarray(inp["x"], np.float32)
    wts = {n: np.ascontiguousarray(np.asarray(inp[n], np.float32)) for n, _ in WSHAPES}
    in_maps = []
    for core in range(8):
        b, half = core // 2, core % 2
        xs = np.zeros((T, D), np.float32)
        if half == 0:
            xs[HALO:] = x[b, 0:TCORE]
        else:
            xs[:] = x[b, TCORE - HALO:2 * TCORE]
        m = {"xT": np.ascontiguousarray(xs.T), "par": build_params(inp, half)}
        m.update(wts)
        in_maps.append(m)
    return in_maps


def kernel(**inp):
    in_maps = make_in_maps(inp)
    nc = build_program()
    res = run_bass_kernel_spmd(nc, in_maps, core_ids=list(range(8)))
    out = np.empty((BATCH, SEQ, D), np.float32)
    for core in range(8):
        b, half = core // 2, core % 2
        out[b, half * TCORE:(half + 1) * TCORE] = np.asarray(res.results[core]["yT"]).T
    return out
```

```python
import numpy as np
from contextlib import ExitStack
import concourse.bass as bass
import concourse.mybir as mybir
from concourse.bass_utils import run_bass_kernel_spmd

F32 = mybir.dt.float32
BF16 = mybir.dt.bfloat16
ALU = mybir.AluOpType
AF = mybir.ActivationFunctionType

D = 1024
KC = 8
FF = 2816
FC = 22
SEQ = 4096
BATCH = 4
TCORE = 2048
HALO = 64
T = TCORE + HALO
PAD = 32
ZT = T + PAD
TILES = [(0, 424), (424, 424), (848, 424), (1272, 424), (1696, 416)]
NT = len(TILES)
POOL_W = (2, 4, 8, 16)
NTAB = HALO + 16
CONFK = 31
FGROUPS = [(0, 8), (8, 15), (15, 22)]

_PL = {}
_off = 0
for _n, _c in ([(f"ln1_{i}", 8) for i in range(4)] + [(f"ln2_{i}", 8) for i in range(4)] + [("ln_f", 8)] +
               [("a0_conv", 24), ("a3_conv", 24), ("b1_scale", 8), ("c2_b_pw1", 16), ("c2_dw", 8 * CONFK),
                ("c2_b_dw", 8), ("c2_ln_g", 8), ("c2_ln_b", 8), ("c2_b_pw2", 8), ("eps_rms", 1), ("eps_ln", 1),
                ("mask", HALO), ("tab", 4 * NTAB), ("ident", 128)]):
    _PL[_n] = (_off, _c)
    _off += _c
NPAR = _off

WSHAPES = []
for _i in (0, 3):
    WSHAPES += [(f"a{_i}_w_in", [D, 3 * D]), (f"a{_i}_w_out", [D, D])]
WSHAPES += [("b1_w_grp", [4, 256, 256]), ("c2_w_pw1", [D, 2 * D]), ("c2_w_pw2", [D, D])]
for _i in range(4):
    WSHAPES += [(f"ffn{_i}_w_gu", [D, 2 * FF]), (f"ffn{_i}_w_down", [FF, D])]


def _vec(v, n):
    return np.ascontiguousarray(np.asarray(v, np.float32).reshape(n, 128).T)


def build_params(inp, half):
    P = np.zeros((128, NPAR), np.float32)

    def put(name, arr):
        o, c = _PL[name]
        P[:, o:o + c] = np.asarray(arr, np.float32).reshape(128, c)

    for i in range(4):
        put(f"ln1_{i}", _vec(inp[f"ln1_{i}"], 8))
        put(f"ln2_{i}", _vec(inp[f"ln2_{i}"], 8))
    put("ln_f", _vec(inp["ln_f"], 8))
    for nm in ("a0_conv", "a3_conv"):
        w = np.asarray(inp[nm], np.float32)
        put(nm, w.reshape(3, 8, 128).transpose(2, 1, 0))
    put("b1_scale", _vec(inp["b1_scale"], 8))
    put("c2_b_pw1", _vec(inp["c2_b_pw1"], 16))
    put("c2_dw", np.asarray(inp["c2_dw"], np.float32).reshape(CONFK, 8, 128).transpose(2, 1, 0))
    for nm in ("c2_b_dw", "c2_ln_g", "c2_ln_b", "c2_b_pw2"):
        put(nm, _vec(inp[nm], 8))
    put("eps_rms", np.full((128, 1), 1e-6, np.float32))
    put("eps_ln", np.full((128, 1), 1e-5, np.float32))
    put("mask", np.full((128, HALO), 1.0 if half == 1 else 0.0, np.float32))
    tab = np.zeros((4, NTAB), np.float32)
    for g, w in enumerate(POOL_W):
        for j in range(NTAB):
            t_abs = half * TCORE + j - HALO
            cnt = min(t_abs + 1, w) if t_abs >= 0 else w
            tab[g, j] = 1.0 / cnt
    put("tab", np.broadcast_to(tab.reshape(1, 4 * NTAB), (128, 4 * NTAB)))
    put("ident", np.eye(128, dtype=np.float32))
    return P


class Sched:
    def __init__(self, nc):
        self.nc = nc
        self.ops = []
        self.last_w = {}
        self.readers = {}
        self.phase = 0

    def op(self, eng, fn, reads=(), writes=(), dma=None):
        i = len(self.ops)
        deps = set()
        for r in reads:
            if r in self.last_w:
                deps.add(self.last_w[r])
        for r in writes:
            if r in self.last_w:
                deps.add(self.last_w[r])
            rd = self.readers.get(r)
            if rd:
                deps.update(rd.values())
        key = eng if dma is None else ("dma", i)
        for r in reads:
            self.readers.setdefault(r, {})[key] = i
        for r in writes:
            self.last_w[r] = i
            self.readers[r] = {}
        deps.discard(i)
        if eng == "pe":
            deps = {d for d in deps if not (self.ops[d]["eng"] == "pe" and self.ops[d]["dma"] is None)}
        self.ops.append(dict(eng=eng, fn=fn, deps=deps, dma=dma, need=False, phase=self.phase))
        return i

    def emit(self):
        nc = self.nc
        ops = self.ops
        for o in ops:
            for d in o["deps"]:
                ops[d]["need"] = True
            if o["dma"] is not None:
                o["need"] = True
        last_on = {}
        for i, o in enumerate(ops):
            last_on[o["eng"]] = i
        for i in last_on.values():
            ops[i]["need"] = True
        cnt = {}
        for o in ops:
            if not o["need"]:
                o["ev"] = None
                continue
            if o["dma"] is not None:
                s = "d_" + o["dma"]
                cnt[s] = cnt.get(s, 0) + 16
            else:
                s = "e%d_%s" % (o["phase"], o["eng"])
                cnt[s] = cnt.get(s, 0) + 1
            o["ev"] = (s, cnt[s])
        self.counts = dict(cnt)

        def merge(a, b):
            for s_, v_ in b.items():
                if v_ > a.get(s_, 0):
                    a[s_] = v_

        K = [None] * len(ops)
        issue_known = {}
        last_c = {}
        nwait = 0
        for i, o in enumerate(ops):
            E = o["eng"]
            tmp = issue_known.setdefault(E, {})
            waits = {}
            for d in sorted(o["deps"], reverse=True):
                s_, v_ = ops[d]["ev"]
                if tmp.get(s_, 0) >= v_:
                    continue
                if v_ > waits.get(s_, 0):
                    waits[s_] = v_
                merge(tmp, K[d])
            o["waits"] = waits
            nwait += len(waits)
            k = dict(tmp)
            if o["dma"] is None:
                if E in last_c:
                    merge(k, K[last_c[E]])
                last_c[E] = i
            if o["ev"] is not None:
                if o["ev"][1] > k.get(o["ev"][0], 0):
                    k[o["ev"][0]] = o["ev"][1]
            K[i] = k
        self.nwait = nwait
        with ExitStack() as es:
            sems = {s: es.enter_context(nc.semaphore(s)) for s in sorted(cnt)}
            block = es.enter_context(nc.Block())

            def make(engname):
                def body(eng):
                    for o in ops:
                        if o["eng"] != engname:
                            continue
                        for s, v in o["waits"].items():
                            eng.wait_ge(sems[s], v)
                        ins = o["fn"](eng)
                        if o["ev"] is not None:
                            ins.then_inc(sems[o["ev"][0]], 16 if o["dma"] is not None else 1)
                    if engname == "sp":
                        known = issue_known.get("sp", {})
                        for s, v in cnt.items():
                            if v > known.get(s, 0):
                                eng.wait_ge(sems[s], v)
                return body

            block.tensor(make("pe"))
            block.scalar(make("act"))
            block.vector(make("dve"))
            block.gpsimd(make("pool"))
            block.sync(make("sp"))


class Rot:
    def __init__(self, name, bufs):
        self.name, self.bufs, self.i = name, bufs, 0

    def next(self):
        j = self.i % len(self.bufs)
        self.i += 1
        return (self.name, j), self.bufs[j]


def f_mm(out, lhsT, rhs, start, stop):
    return lambda e: e.matmul(out, lhsT=lhsT, rhs=rhs, start=start, stop=stop)


def f_act(out, in_, func, bias=None, scale=None):
    kw = {}
    if bias is not None:
        kw["bias"] = bias
    if scale is not None:
        kw["scale"] = scale
    return lambda e: e.activation(out=out, in_=in_, func=func, **kw)


def f_tt(out, in0, in1, op):
    return lambda e: e.tensor_tensor(out=out, in0=in0, in1=in1, op=op)


def f_stt(out, in0, scalar, in1, op0, op1):
    return lambda e: e.scalar_tensor_tensor(out=out, in0=in0, scalar=scalar, in1=in1, op0=op0, op1=op1)


def f_copy(out, in_):
    return lambda e: e.tensor_copy(out=out, in_=in_)


def f_recip(out, in_):
    return lambda e: e.reciprocal(out=out, in_=in_)


def f_memset(ap, v):
    return lambda e: e.memset(ap, v)


def f_dma(out, in_):
    return lambda e: e.dma_start(out=out, in_=in_)


def build_program(debug=False, nlayers=4):
    nc = bass.Bass("TRN2", target_bir_lowering=False)
    xT = nc.dram_tensor("xT", [D, T], F32, kind="ExternalInput").ap()
    par = nc.dram_tensor("par", [128, NPAR], F32, kind="ExternalInput").ap()
    W = {n: nc.dram_tensor(n, s, F32, kind="ExternalInput").ap() for n, s in WSHAPES}
    yT = nc.dram_tensor("yT", [D, TCORE], F32, kind="ExternalOutput").ap()
    dbg = nc.dram_tensor("dbg", [8, D, T], F32, kind="ExternalOutput").ap() if debug else None

    with ExitStack() as es:
        def sb(name, shape, dt):
            return es.enter_context(nc.sbuf_tensor(name, shape, dt))

        H = sb("H", [128, KC, T], F32)
        Uf = sb("U", [128, KC * T], BF16)
        Zf = sb("Z", [128, KC * ZT], BF16)
        WS = [sb(f"ws{i}", [128, 4096], BF16) for i in range(4)]
        PAR = sb("PAR", [128, NPAR], F32)
        ONES = sb("ones", [128, 128], BF16)
        IDB = sb("idb", [128, 128], BF16)
        TMPb = [sb(f"tmp{i}", [128, 544], F32) for i in range(5)]
        SQb = [sb(f"sq{i}", [128, 512], BF16) for i in range(4)]
        RSb = [sb(f"rs{i}", [128, 512], F32) for i in range(3)]
        DGb = [sb(f"dg{i}", [128, CONFK, 128], BF16) for i in range(2)]
        SCR = sb("scr", [128, 8], F32)
        PS = {n: es.enter_context(nc.psum_tensor("ps" + n, [128, 512], F32))
              for n in ("G0", "G1", "U0", "U1", "D0", "D1", "N0", "N1")}

        Uv = Uf[:].rearrange("p (k t) -> p k t", k=KC)
        Zv = Zf[:].rearrange("p (k t) -> p k t", k=KC)
        UB = [Uf[:, i * 2 * T:(i + 1) * 2 * T].bitcast(F32) for i in range(4)]
        YT = [WS[2][:].bitcast(F32), WS[3][:].bitcast(F32)]

        def zt(k, c0, n):
            return Zv[:, k, PAD + c0:PAD + c0 + n]

        def tb(ti, lo):
            c0, n = TILES[ti]
            if ti == 0:
                return lo, c0 + n - lo
            return c0, n

        def ut(k, c0, n):
            return Uv[:, k, c0:c0 + n]

        def pc(name, c=0, n=1):
            o, _ = _PL[name]
            return PAR[:, o + c:o + c + n]

        S = Sched(nc)
        TMP = Rot("tmp", TMPb)
        SQ = Rot("sq", SQb)
        RS = Rot("rs", RSb)
        DG = Rot("dg", DGb)
        NB = Rot("psN", [PS["N0"], PS["N1"]])
        slot_i = [0]

        def next_slot():
            s = slot_i[0] % 4
            slot_i[0] += 1
            return s

        HALL = lambda k: [("h", k, ti) for ti in range(NT)]
        UALL = [("u", k, ti) for k in range(KC) for ti in range(NT)]

        S.op("sp", f_dma(PAR[:], par), writes=["par"], dma="par")
        xv = xT.rearrange("(k p) t -> p k t", p=128)
        for ti in range(NT):
            c0, n = TILES[ti]
            S.op("sp", f_dma(H[:, :, c0:c0 + n], xv[:, :, c0:c0 + n]), writes=[("h", k, ti) for k in range(KC)], dma=f"x{ti}")
        S.op("dve", f_memset(ONES[:], 1.0 / D), writes=["ones"])
        S.op("dve", f_copy(IDB[:], pc("ident", 0, 128)), reads=["par"], writes=["idb"])
        S.op("dve", f_memset(Zv[:, :, 0:PAD], 0.0), writes=["zpad"])

        def WSR(s):
            return [("ws", s, 0), ("ws", s, 1), ("ws", s, 2)]

        def load_w(dst, srcs):
            s = next_slot()
            d = dst(WS[s])
            n = len(srcs)
            for i, (pf, src) in enumerate(srcs):
                wr = [("ws", s, i)] + ([("ws", s, q) for q in range(n, 3)] if i == n - 1 else [])
                S.op("pool", f_dma(pf(d), src), writes=wr, dma=f"ws{s}")
            return s, d

        def recip(rr, ra):
            S.op("dve", f_recip(ra, ra), reads=[rr], writes=[rr])

        def norm_stats(ti, dst=None, lo=0):
            c0, n = tb(ti, lo)
            pr, pn = NB.next()
            for k in range(KC):
                sr, sa = SQ.next()
                S.op("act", f_act(sa[:, :n], H[:, k, c0:c0 + n], AF.Square), reads=[("h", k, ti)], writes=[sr])
                S.op("pe", f_mm(pn[:, :n], ONES[:], sa[:, :n], k == 0, k == KC - 1), reads=[sr, "ones"], writes=[pr])
            if dst is None:
                rr, ra = RS.next()
                ra = ra[:, :n]
            else:
                rr, ra = dst(c0, n)
            S.op("act", f_act(ra, pn[:, :n], AF.Sqrt, bias=pc("eps_rms"), scale=1.0), reads=[pr, "par"], writes=[rr])
            return rr, ra, (lambda: recip(rr, ra))

        def norm_cb_U(gname, lo=0):
            def cb(ti):
                c0, n = tb(ti, lo)
                rr, ra, th = norm_stats(ti, lo=lo)
                ths = [th]
                for k in range(KC):
                    ths.append(lambda k=k: S.op(
                        "dve", f_stt(ut(k, c0, n), H[:, k, c0:c0 + n], pc(gname, k), ra, ALU.mult, ALU.mult),
                        reads=[("h", k, ti), rr, "par"], writes=[("u", k, ti)]))
                return ths
            return cb

        def run_all(ths):
            for th in ths:
                th()

        def dump(stage):
            if debug:
                for k in range(KC):
                    S.op("sp", f_dma(dbg[stage, k * 128:(k + 1) * 128, :], H[:, k, :]), reads=HALL(k), dma=f"dbg{k}")

        par_i = [0]

        def parity():
            par_i[0] ^= 1
            return par_i[0]

        def proj_add(wname, row0, nk, src_reg, src_ap, bias_name=None, next_cb=None, lo=0, pre=()):
            def load(mb):
                return load_w(lambda ws: ws[:, 0:nk * 512].rearrange("p (kc n) -> p kc n", kc=nk),
                              [(lambda dd: dd, W[wname][row0:row0 + nk * 128, mb * 512:(mb + 1) * 512].rearrange("(kc p) n -> p kc n", p=128))])

            def group(s, d, mb, mi, ti):
                m = mb * 4 + mi
                c0, n = tb(ti, lo)
                p = parity()
                pd = PS[f"D{p}"]
                for k in range(nk):
                    S.op("pe", f_mm(pd[:, :n], d[:, k, mi * 128:(mi + 1) * 128], src_ap(k, c0, n), k == 0, k == nk - 1),
                         reads=WSR(s) + [(src_reg, k, ti)], writes=[("ps", f"D{p}")])
                hh = H[:, m, c0:c0 + n]
                if bias_name is not None:
                    fn = f_stt(hh, pd[:, :n], pc(bias_name, m), hh, ALU.add, ALU.add)
                else:
                    fn = f_tt(hh, hh, pd[:, :n], ALU.add)
                S.op("dve", fn, reads=[("ps", f"D{p}"), ("h", m, ti), "par"], writes=[("h", m, ti)])

            if next_cb is None:
                run_all(pre)
                for mb in range(2):
                    s, d = load(mb)
                    for mi in range(4):
                        for ti in range(NT):
                            group(s, d, mb, mi, ti)
            else:
                blocks = [load(0), load(1)]
                deferred = list(pre)
                for ti in range(NT):
                    for mb in range(2):
                        for mi in range(4):
                            group(blocks[mb][0], blocks[mb][1], mb, mi, ti)
                            for _ in range(2):
                                if deferred:
                                    deferred.pop(0)()
                    run_all(deferred)
                    deferred = []
                    if ti >= 1:
                        deferred = list(next_cb(ti - 1))
                run_all(deferred)
                run_all(next_cb(NT - 1))

        def ffn(L, next_cb, lo=0):
            gu = W[f"ffn{L}_w_gu"].rearrange("(kc p) (two f) -> p kc two f", p=128, two=2)
            for gi, (j0, j1) in enumerate(FGROUPS):
                for jp in range(j0, j1, 2):
                    nch = min(2, j1 - jp)
                    ncol = nch * 128
                    s, d = load_w(lambda ws: ws[:, 0:KC * 2 * ncol].rearrange("p (kc two f) -> p kc two f", kc=KC, two=2),
                                  [(lambda dd, q=q: dd[:, :, q, :], gu[:, :, q, jp * 128:jp * 128 + ncol]) for q in range(2)])
                    for jj in range(nch):
                        j = jp + jj
                        for ti in range(NT):
                            c0, n = tb(ti, lo)
                            p = parity()
                            pg, pu = PS[f"G{p}"], PS[f"U{p}"]
                            for k in range(KC):
                                S.op("pe", f_mm(pg[:, :n], d[:, k, 0, jj * 128:(jj + 1) * 128], ut(k, c0, n), k == 0, k == KC - 1),
                                     reads=WSR(s) + [("u", k, ti)], writes=[("ps", f"G{p}")])
                            for k in range(KC):
                                S.op("pe", f_mm(pu[:, :n], d[:, k, 1, jj * 128:(jj + 1) * 128], ut(k, c0, n), k == 0, k == KC - 1),
                                     reads=WSR(s) + [("u", k, ti)], writes=[("ps", f"U{p}")])
                            tr, ta = TMP.next()
                            S.op("act", f_act(ta[:, :n], pg[:, :n], AF.Silu), reads=[("ps", f"G{p}")], writes=[tr])
                            S.op("dve", f_tt(zt(j - j0, c0, n), ta[:, :n], pu[:, :n], ALU.mult),
                                 reads=[tr, ("ps", f"U{p}")], writes=[("z", j - j0, ti)])
                proj_add(f"ffn{L}_w_down", j0 * 128, j1 - j0, "z", zt,
                         next_cb=(next_cb if gi == len(FGROUPS) - 1 else None), lo=lo)

        def mixer_a(L, next_cb, lo=0, lo_out=0):
            w_in = W[f"a{L}_w_in"].rearrange("(kc p) (three f) -> p kc three f", p=128, three=3)
            cname = f"a{L}_conv"
            for j in range(KC):
                s, d = load_w(lambda ws: ws[:, 0:KC * 3 * 128].rearrange("p (kc three f) -> p kc three f", kc=KC, three=3),
                              [(lambda dd, q=q: dd[:, :, q, :], w_in[:, :, q, j * 128:(j + 1) * 128]) for q in range(3)])
                prev = None
                for ti in range(NT):
                    c0, n = tb(ti, lo)
                    p = parity()
                    banks = [PS[f"D{p}"], PS[f"G{p}"], PS[f"U{p}"]]
                    bnames = [("ps", f"D{p}"), ("ps", f"G{p}"), ("ps", f"U{p}")]
                    for which in (1, 2, 0):
                        for k in range(KC):
                            S.op("pe", f_mm(banks[which][:, :n], d[:, k, which, :], ut(k, c0, n), k == 0, k == KC - 1),
                                 reads=WSR(s) + [("u", k, ti)], writes=[bnames[which]])
                    cr, ca = TMP.next()
                    S.op("act", f_act(ca[:, :n], banks[1][:, :n], AF.Copy), reads=[bnames[1]], writes=[cr])
                    vr, va = TMP.next()
                    if prev is None:
                        S.op("dve", f_memset(va[:, 0:2], 0.0), writes=[vr])
                    else:
                        pr_, pa_, pn_ = prev
                        S.op("dve", f_copy(va[:, 0:2], pa_[:, pn_:pn_ + 2]), reads=[pr_], writes=[vr])
                    S.op("dve", f_tt(va[:, 2:2 + n], ca[:, :n], banks[2][:, :n], ALU.mult), reads=[cr, bnames[2], vr], writes=[vr])
                    if ti == 0 and lo < HALO:
                        hw = HALO - lo
                        S.op("dve", f_tt(va[:, 2:2 + hw], va[:, 2:2 + hw], pc("mask", 0, hw), ALU.mult), reads=[vr, "par"], writes=[vr])
                    ar, aa = TMP.next()
                    S.op("act", f_act(aa[:, :n], va[:, 0:n], AF.Identity, scale=pc(cname, 3 * j + 0)), reads=[vr, "par"], writes=[ar])
                    S.op("dve", f_stt(aa[:, :n], va[:, 1:1 + n], pc(cname, 3 * j + 1), aa[:, :n], ALU.mult, ALU.add), reads=[vr, ar, "par"], writes=[ar])
                    S.op("dve", f_stt(aa[:, :n], va[:, 2:2 + n], pc(cname, 3 * j + 2), aa[:, :n], ALU.mult, ALU.add), reads=[vr, ar, "par"], writes=[ar])
                    S.op("dve", f_tt(zt(j, c0, n), aa[:, :n], banks[0][:, :n], ALU.mult), reads=[ar, bnames[0]], writes=[("z", j, ti)])
                    prev = (vr, va, n)
            proj_add(f"a{L}_w_out", 0, KC, "z", zt, next_cb=next_cb, lo=lo_out)

        ubn = ["ub0", "ub1", "ub2", "ub3"]

        def norm_cb_b(ti):
            if ti == 0:
                S.op("dve", f_memset(SCR[:, 0:1], 0.0), writes=UALL + ubn + ["scr"])
            rr, ra, th = norm_stats(ti, dst=(lambda c0, n: ("ub0", UB[0][:, c0:c0 + n])))
            return [th]

        def mixer_b(next_cb):
            grp = W["b1_w_grp"].rearrange("g (kc p) n -> p g kc n", p=128)
            s, d = load_w(lambda ws: ws[:, 0:2048].rearrange("p (g kc n) -> p g kc n", g=4, kc=2),
                          [(lambda dd, q=q: dd[:, :, q, :], grp[:, :, q, :]) for q in range(2)])
            uc, sA, sB = UB[1], UB[2], UB[3]
            for c in range(KC):
                g = c // 2
                w = POOL_W[g]
                S.op("dve", f_stt(uc[:, :], H[:, c, :], pc("ln1_1", c), UB[0][:, :], ALU.mult, ALU.mult),
                     reads=HALL(c) + ["ub0", "par"], writes=["ub1"])
                S.op("dve", f_tt(uc[:, 0:HALO], uc[:, 0:HALO], pc("mask", 0, HALO), ALU.mult), reads=["ub1", "par"], writes=["ub1"])
                cur, curn = uc, "ub1"
                dd = 1
                tgt = [(sA, "ub2"), (sB, "ub3")]
                step = 0
                while dd < w:
                    nx, nxn = tgt[step % 2]
                    S.op("dve", f_tt(nx[:, dd:T], cur[:, dd:T], cur[:, 0:T - dd], ALU.add), reads=[curn], writes=[nxn])
                    S.op("act", f_act(nx[:, 0:dd], cur[:, 0:dd], AF.Copy), reads=[curn], writes=[nxn])
                    cur, curn = nx, nxn
                    dd *= 2
                    step += 1
                zall = [("z", c, ti) for ti in range(NT)]
                S.op("dve", f_stt(Zv[:, c, PAD:PAD + T], cur[:, :], 1.0 / w, uc[:, :], ALU.mult, ALU.subtract),
                     reads=[curn, "ub1"], writes=zall)
                tr, ta = TMP.next()
                o, _ = _PL["tab"]
                S.op("dve", f_tt(ta[:, 0:NTAB], cur[:, 0:NTAB], PAR[:, o + g * NTAB:o + (g + 1) * NTAB], ALU.mult), reads=[curn, "par"], writes=[tr])
                S.op("dve", f_tt(Zv[:, c, PAD:PAD + NTAB], ta[:, 0:NTAB], uc[:, 0:NTAB], ALU.subtract), reads=[tr, "ub1"],
                     writes=[("z", c, 0)])
            S.op("dve", f_memset(SCR[:, 0:1], 0.0), writes=UALL + ubn + ["scr"])
            for g in range(4):
                last = (g == 3 and next_cb is not None)
                if last:
                    order = [(ti, mo) for ti in range(NT) for mo in range(2)]
                else:
                    order = [(ti, mo) for mo in range(2) for ti in range(NT)]
                for (ti, mo) in order:
                    m = 2 * g + mo
                    c0, n = TILES[ti]
                    p = parity()
                    pd = PS[f"D{p}"]
                    for kc in range(2):
                        S.op("pe", f_mm(pd[:, :n], d[:, g, kc, mo * 128:(mo + 1) * 128], zt(2 * g + kc, c0, n), kc == 0, kc == 1),
                             reads=WSR(s) + [("z", 2 * g + kc, ti)], writes=[("ps", f"D{p}")])
                    hh = H[:, m, c0:c0 + n]
                    S.op("dve", f_stt(hh, pd[:, :n], pc("b1_scale", m), hh, ALU.mult, ALU.add),
                         reads=[("ps", f"D{p}"), ("h", m, ti), "par"], writes=[("h", m, ti)])
                    if last and mo == 1 and ti >= 1:
                        run_all(next_cb(ti - 1))
                if last:
                    run_all(next_cb(NT - 1))

        def mixer_c(next_cb, lo1=0, lo2=0):
            pw1 = W["c2_w_pw1"].rearrange("(kc p) (two f) -> p kc two f", p=128, two=2)
            for jp in range(0, KC, 2):
                s, d = load_w(lambda ws: ws[:, 0:KC * 2 * 256].rearrange("p (kc two f) -> p kc two f", kc=KC, two=2),
                              [(lambda dd, q=q: dd[:, :, q, :], pw1[:, :, q, jp * 128:jp * 128 + 256]) for q in range(2)])
                for jj in range(2):
                    j = jp + jj
                    for ti in range(NT):
                        c0, n = tb(ti, lo1)
                        p = parity()
                        pa, pg = PS[f"G{p}"], PS[f"U{p}"]
                        for k in range(KC):
                            S.op("pe", f_mm(pa[:, :n], d[:, k, 0, jj * 128:(jj + 1) * 128], ut(k, c0, n), k == 0, k == KC - 1),
                                 reads=WSR(s) + [("u", k, ti)], writes=[("ps", f"G{p}")])
                        for k in range(KC):
                            S.op("pe", f_mm(pg[:, :n], d[:, k, 1, jj * 128:(jj + 1) * 128], ut(k, c0, n), k == 0, k == KC - 1),
                                 reads=WSR(s) + [("u", k, ti)], writes=[("ps", f"U{p}")])
                        tr, ta = TMP.next()
                        S.op("act", f_act(ta[:, :n], pg[:, :n], AF.Sigmoid, bias=pc("c2_b_pw1", 8 + j), scale=1.0),
                             reads=[("ps", f"U{p}"), "par"], writes=[tr])
                        S.op("dve", f_stt(zt(j, c0, n), pa[:, :n], pc("c2_b_pw1", j), ta[:, :n], ALU.add, ALU.mult),
                             reads=[tr, ("ps", f"G{p}"), "par"], writes=[("z", j, ti)])
                        if ti == 0 and lo1 < HALO:
                            hw = HALO - lo1
                            S.op("dve", f_tt(zt(j, lo1, hw), zt(j, lo1, hw), pc("mask", 0, hw), ALU.mult), reads=[("z", j, ti), "par"],
                                 writes=[("z", j, ti)])
            dwo, _ = _PL["c2_dw"]
            pmr, pm = ("ps", "N0"), PS["N0"]
            pqr, pq = ("ps", "N1"), PS["N1"]
            steps = [(ti, c) for ti in range(NT) for c in range(KC)]
            diags = {}

            def build_diag(si):
                c = steps[si][1]
                dr, da = DG.next()
                S.op("dve", f_tt(da[:], IDB[:].unsqueeze(1).to_broadcast([128, CONFK, 128]),
                                 PAR[:, dwo + c * CONFK:dwo + (c + 1) * CONFK].unsqueeze(2).to_broadcast([128, CONFK, 128]), ALU.mult),
                     reads=["idb", "par"], writes=[dr])
                diags[si] = (dr, da)

            def ytile(c, n):
                return WSR(2 + c // 4), YT[c // 4][:, (c % 4) * 512:(c % 4) * 512 + n]

            def stats_mm(pend):
                ti, c, s1r, s1, s2r, s2 = pend
                n = tb(ti, lo2)[1]
                S.op("pe", f_mm(pm[:, :n], ONES[:], s1[:, :n], c == 0, c == KC - 1), reads=[s1r, "ones"], writes=[pmr])
                S.op("pe", f_mm(pq[:, :n], ONES[:], s2[:, :n], c == 0, c == KC - 1), reads=[s2r, "ones"], writes=[pqr])

            def ln_pre(ti):
                c0, n = tb(ti, lo2)
                mur, mu = RS.next()
                mu = mu[:, :n]
                S.op("dve", f_copy(mu, pm[:, :n]), reads=[pmr], writes=[mur])
                rr, ra = RS.next()
                ra = ra[:, :n]
                S.op("dve", f_tt(ra, mu, mu, ALU.mult), reads=[mur], writes=[rr])
                S.op("dve", f_tt(ra, pq[:, :n], ra, ALU.subtract), reads=[pqr, rr], writes=[rr])
                S.op("act", f_act(ra, ra, AF.Sqrt, bias=pc("eps_ln"), scale=1.0), reads=[rr, "par"], writes=[rr])
                recip(rr, ra)
                return (ti, mur, mu, rr, ra)

            def ln_chunk(st, c):
                ti, mur, mu, rr, ra = st
                c0, n = tb(ti, lo2)
                ys, ya = ytile(c, n)
                tr, ta = TMP.next()
                S.op("dve", f_tt(ta[:, :n], ya, mu, ALU.subtract), reads=[("yt", c), mur] + ys, writes=[tr])
                S.op("dve", f_tt(ta[:, :n], ta[:, :n], ra, ALU.mult), reads=[tr, rr], writes=[tr])
                S.op("act", f_act(ut(c, c0, n), ta[:, :n], AF.Silu, bias=pc("c2_ln_b", c), scale=pc("c2_ln_g", c)),
                     reads=[tr, "par"], writes=[("u", c, ti)])

            build_diag(0)
            pend = None
            lnst = None
            for si, (ti, c) in enumerate(steps):
                c0, n = tb(ti, lo2)
                if si + 1 < len(steps):
                    build_diag(si + 1)
                dr, da = diags[si]
                p = parity()
                pd = PS[f"D{p}"]
                zr = [("z", c, ti)] + ([("z", c, ti - 1)] if ti > 0 else ["zpad"])
                for k in range(CONFK):
                    a0 = PAD + c0 - (CONFK - 1) + k
                    S.op("pe", f_mm(pd[:, :n], da[:, k, :], Zv[:, c, a0:a0 + n], k == 0, k == CONFK - 1),
                         reads=[dr] + zr, writes=[("ps", f"D{p}")])
                if pend is not None:
                    stats_mm(pend)
                    if pend[1] == KC - 1:
                        lnst = ln_pre(pend[0])
                if lnst is not None:
                    ln_chunk(lnst, c)
                ys, ya = ytile(c, n)
                bb = pc("c2_b_dw", c)
                S.op("act", f_act(ya, pd[:, :n], AF.Identity, bias=bb, scale=1.0), reads=[("ps", f"D{p}"), "par"], writes=ys + [("yt", c)])
                s1r, s1 = SQ.next()
                S.op("act", f_act(s1[:, :n], pd[:, :n], AF.Identity, bias=bb, scale=1.0), reads=[("ps", f"D{p}"), "par"], writes=[s1r])
                s2r, s2 = SQ.next()
                S.op("act", f_act(s2[:, :n], pd[:, :n], AF.Square, bias=bb, scale=1.0), reads=[("ps", f"D{p}"), "par"], writes=[s2r])
                pend = (ti, c, s1r, s1, s2r, s2)
            stats_mm(pend)
            lnst = ln_pre(pend[0])
            pre = [(lambda c=c: ln_chunk(lnst, c)) for c in range(KC)]
            slot_i[0] = 0
            proj_add("c2_w_pw2", 0, KC, "u", ut, bias_name="c2_b_pw2", next_cb=next_cb, lo=lo2, pre=pre)

        def final_cb(ti):
            c0, n = tb(ti, HALO)
            lo = c0
            rr, ra, th = norm_stats(ti, lo=HALO)
            ths = [th]

            def out_k(k):
                tr, ta = TMP.next()
                S.op("dve", f_stt(ta[:, :n], H[:, k, c0:c0 + n], pc("ln_f", k), ra, ALU.mult, ALU.mult),
                     reads=[("h", k, ti), rr, "par"], writes=[tr])
                S.op("sp", f_dma(yT[k * 128:(k + 1) * 128, lo - HALO:c0 + n - HALO], ta[:, lo - c0:n]), reads=[tr], dma=f"o{tr[1]}")
            for k in range(KC):
                ths.append(lambda k=k: out_k(k))
            return ths

        LO_FFN = [0, 24, 56, 64]
        phases = []
        mix_norm = [norm_cb_U("ln1_0", 0), norm_cb_b, norm_cb_U("ln1_2", 24), norm_cb_U("ln1_3", 56)]
        mix_body = [lambda cb: mixer_a(0, cb, 0, 0), mixer_b, lambda cb: mixer_c(cb, 24, 56), lambda cb: mixer_a(3, cb, 56, 64)]
        for L in range(nlayers):
            phases.append((L, mix_norm[L], mix_body[L]))
            phases.append((L, norm_cb_U(f"ln2_{L}", LO_FFN[L]), (lambda cb, L=L: ffn(L, cb, LO_FFN[L]))))
        if nlayers == 4:
            phases.append((4, final_cb, None))
        for ti in range(NT):
            run_all(phases[0][1](ti))
        for i, (ph, ncb, body) in enumerate(phases):
            S.phase = ph
            if body is None:
                break
            nxt = phases[i + 1][1] if i + 1 < len(phases) else None
            body(nxt)
            dump(i)
        S.emit()
        nc._sched_counts = S.counts
        nc._n_ops = len(S.ops)
        nc._nwait = S.nwait
    return nc


def make_in_maps(inp):
    x = np.asarray(inp["x"], np.float32)
    wts = {n: np.ascontiguousarray(np.asarray(inp[n], np.float32)) for n, _ in WSHAPES}
    in_maps = []
    for core in range(8):
        b, half = core // 2, core % 2
        xs = np.zeros((T, D), np.float32)
        if half == 0:
            xs[HALO:] = x[b, 0:TCORE]
        else:
            xs[:] = x[b, TCORE - HALO:2 * TCORE]
        m = {"xT": np.ascontiguousarray(xs.T), "par": build_params(inp, half)}
        m.update(wts)
        in_maps.append(m)
    return in_maps


def kernel(**inp):
    in_maps = make_in_maps(inp)
    nc = build_program()
    res = run_bass_kernel_spmd(nc, in_maps, core_ids=list(range(8)))
    out = np.empty((BATCH, SEQ, D), np.float32)
    for core in range(8):
        b, half = core // 2, core % 2
        out[b, half * TCORE:(half + 1) * TCORE] = np.asarray(res.results[core]["yT"]).T
    return out
```
